# Optimizing a Trainium2 kernel written in Bass

```python
import jax, jax.numpy as jnp
from jax import lax
import numpy as np

D_MODEL = 1024
BATCH = 4
SEQ = 4096
DEPTH = 1

D_PLE = 256
D_MIX = 2 * D_MODEL
D_SSM = D_MIX // 2
D_CONF = D_MIX - D_SSM
SSM_HEAD_DIM = 64
SSM_HEADS = D_SSM // SSM_HEAD_DIM
SSM_GROUPS = 2
SSM_HPG = SSM_HEADS // SSM_GROUPS
SSM_STATE = 128
SSM_CONV_K = 4
CHUNK = 128
CONF_K = 31
LN_EPS = 1e-5
RMS_EPS = 1e-5

COLS = (D_SSM,
        D_SSM,
        SSM_GROUPS * SSM_STATE,
        SSM_GROUPS * SSM_STATE,
        SSM_HEADS,
        2 * D_CONF,
        D_CONF)
D_IN_PROJ = sum(COLS)
SPLITS = tuple(int(s) for s in np.cumsum(COLS)[:-1])

DEEPNORM_ALPHA = (2.0 * DEPTH) ** 0.25
DEEPNORM_BETA = (8.0 * DEPTH) ** -0.25

kernel_name = "hybrid_ssd_conformer_deepnorm_block"


def layer_norm(x, g, b):
    xf = x.astype(jnp.float32)
    mu = jnp.mean(xf, axis=-1, keepdims=True)
    var = jnp.mean(jnp.square(xf - mu), axis=-1, keepdims=True)
    return ((xf - mu) * lax.rsqrt(var + LN_EPS)).astype(x.dtype) * g + b


def causal_depthwise_conv(x, w, b):
    k = w.shape[0]
    xp = jnp.pad(x, ((0, 0), (k - 1, 0), (0, 0)))
    y = lax.conv_general_dilated(xp, w[:, None, :], window_strides=(1,), padding='VALID',
                                 dimension_numbers=('NWC', 'WIO', 'NWC'),
                                 feature_group_count=x.shape[-1])
    return y + b


def ssd_chunked(xh, dt, a, bm, cm):
    bsz, l, g, j, p = xh.shape
    n = bm.shape[-1]
    c = l // CHUNK
    xdt = (xh * dt[..., None]).reshape(bsz, c, CHUNK, g, j, p)
    adt = jnp.moveaxis((dt * a).reshape(bsz, c, CHUNK, g, j), 2, -1)
    a_cs = jnp.cumsum(adt, axis=-1)
    bc = bm.reshape(bsz, c, CHUNK, g, n)
    cc = cm.reshape(bsz, c, CHUNK, g, n)
    causal = jnp.tril(jnp.ones((CHUNK, CHUNK), dtype=bool))
    seg = a_cs[..., :, None] - a_cs[..., None, :]
    ldec = jnp.exp(jnp.where(causal, seg, -jnp.inf))
    cb = jnp.einsum('bclgn,bcsgn->bcgls', cc, bc)
    y_diag = jnp.einsum('bcgjls,bcsgjp->bclgjp', cb[:, :, :, None] * ldec, xdt)
    decay_states = jnp.exp(a_cs[..., -1:] - a_cs)
    states = jnp.einsum('bclgn,bcgjl,bclgjp->bcgjpn', bc, decay_states, xdt)
    chunk_decay = jnp.exp(a_cs[..., -1])

    def step(h, inp):
        s, d = inp
        return h * d[..., None, None] + s, h

    h0 = jnp.zeros((bsz, g, j, p, n), dtype=xdt.dtype)
    _, prev = lax.scan(step, h0, (jnp.moveaxis(states, 1, 0), jnp.moveaxis(chunk_decay, 1, 0)))
    prev = jnp.moveaxis(prev, 0, 1)
    y_off = jnp.einsum('bclgn,bcgjpn,bcgjl->bclgjp', cc, prev, jnp.exp(a_cs))
    return (y_diag + y_off).reshape(bsz, l, g, j, p)


def setup_inputs(seed: int = 0) -> dict:
    key = jax.random.key(seed)
    ks = jax.random.split(key, 26)
    nrm = jax.random.normal
    f32 = jnp.float32
    x = nrm(ks[0], (BATCH, SEQ, D_MODEL), f32)
    p = nrm(ks[1], (DEPTH, BATCH, SEQ, D_PLE), f32)
    ln_emb_g = 1.0 + 0.02 * nrm(ks[2], (D_MODEL,), f32)
    ln_emb_b = 0.02 * nrm(ks[3], (D_MODEL,), f32)
    w_in = nrm(ks[4], (DEPTH, D_MODEL, D_IN_PROJ), f32) * D_MODEL ** -0.5
    d_xbc = D_SSM + 2 * SSM_GROUPS * SSM_STATE
    ssm_conv_w = nrm(ks[5], (DEPTH, SSM_CONV_K, d_xbc), f32) * SSM_CONV_K ** -0.5
    ssm_conv_b = 0.02 * nrm(ks[6], (DEPTH, d_xbc), f32)
    dt0 = jnp.exp(jax.random.uniform(ks[7], (DEPTH, SSM_HEADS), f32, np.log(1e-3), np.log(1e-1)))
    dt_bias = dt0 + jnp.log(-jnp.expm1(-dt0))
    a_log = jnp.log(jax.random.uniform(ks[8], (DEPTH, SSM_HEADS), f32, 1.0, 16.0))
    d_skip = 1.0 + 0.1 * nrm(ks[9], (DEPTH, SSM_HEADS), f32)
    ssm_norm_g = 1.0 + 0.02 * nrm(ks[10], (DEPTH, D_SSM), f32)
    b_glu = 0.02 * nrm(ks[11], (DEPTH, 2 * D_CONF), f32)
    conf_conv_w = nrm(ks[12], (DEPTH, CONF_K, D_CONF), f32) * CONF_K ** -0.5
    conf_conv_b = 0.02 * nrm(ks[13], (DEPTH, D_CONF), f32)
    conf_ln_g = 1.0 + 0.02 * nrm(ks[14], (DEPTH, D_CONF), f32)
    conf_ln_b = 0.02 * nrm(ks[15], (DEPTH, D_CONF), f32)
    w_out = nrm(ks[16], (DEPTH, D_MIX, D_MODEL), f32) * (DEEPNORM_BETA * (2.0 / (D_MIX + D_MODEL)) ** 0.5)
    b_out = 0.02 * nrm(ks[17], (DEPTH, D_MODEL), f32)
    ln1_g = 1.0 + 0.02 * nrm(ks[18], (DEPTH, D_MODEL), f32)
    ln1_b = 0.02 * nrm(ks[19], (DEPTH, D_MODEL), f32)
    w_ple_gate = nrm(ks[20], (DEPTH, D_MODEL, D_MODEL), f32) * D_MODEL ** -0.5
    w_ple_proj = nrm(ks[21], (DEPTH, D_PLE, D_MODEL), f32) * (DEEPNORM_BETA * (2.0 / (D_PLE + D_MODEL)) ** 0.5)
    ln2_g = 1.0 + 0.02 * nrm(ks[22], (DEPTH, D_MODEL), f32)
    ln2_b = 0.02 * nrm(ks[23], (DEPTH, D_MODEL), f32)
    return {"x": x, "p": p, "ln_emb_g": ln_emb_g, "ln_emb_b": ln_emb_b, "w_in": w_in,
            "ssm_conv_w": ssm_conv_w, "ssm_conv_b": ssm_conv_b, "dt_bias": dt_bias,
            "a_log": a_log, "d_skip": d_skip, "ssm_norm_g": ssm_norm_g, "b_glu": b_glu,
            "conf_conv_w": conf_conv_w, "conf_conv_b": conf_conv_b, "conf_ln_g": conf_ln_g,
            "conf_ln_b": conf_ln_b, "w_out": w_out, "b_out": b_out, "ln1_g": ln1_g,
            "ln1_b": ln1_b, "w_ple_gate": w_ple_gate, "w_ple_proj": w_ple_proj,
            "ln2_g": ln2_g, "ln2_b": ln2_b}


def reference(x, p, ln_emb_g, ln_emb_b, w_in, ssm_conv_w, ssm_conv_b, dt_bias, a_log,
              d_skip, ssm_norm_g, b_glu, conf_conv_w, conf_conv_b, conf_ln_g, conf_ln_b,
              w_out, b_out, ln1_g, ln1_b, w_ple_gate, w_ple_proj, ln2_g, ln2_b):
    bsz, l, _ = x.shape
    f32 = jnp.float32
    gn = SSM_GROUPS * SSM_STATE
    h = layer_norm(x, ln_emb_g, ln_emb_b)
    for i in range(DEPTH):
        proj = jnp.einsum('bld,de->ble', h, w_in[i])
        xs, z, bm, cm, dt_raw, glu, cgate = jnp.split(proj, SPLITS, axis=-1)

        xbc = jnp.concatenate([xs, bm, cm], axis=-1)
        xbc = jax.nn.silu(causal_depthwise_conv(xbc, ssm_conv_w[i], ssm_conv_b[i]))
        xs_c = xbc[..., :D_SSM]
        bm_c = xbc[..., D_SSM:D_SSM + gn].reshape(bsz, l, SSM_GROUPS, SSM_STATE).astype(f32)
        cm_c = xbc[..., D_SSM + gn:].reshape(bsz, l, SSM_GROUPS, SSM_STATE).astype(f32)
        xh = xs_c.reshape(bsz, l, SSM_GROUPS, SSM_HPG, SSM_HEAD_DIM).astype(f32)
        dt = jax.nn.softplus((dt_raw + dt_bias[i]).astype(f32)).reshape(bsz, l, SSM_GROUPS, SSM_HPG)
        a = -jnp.exp(a_log[i].astype(f32)).reshape(SSM_GROUPS, SSM_HPG)
        y = ssd_chunked(xh, dt, a, bm_c, cm_c)
        y = y + d_skip[i].astype(f32).reshape(SSM_GROUPS, SSM_HPG)[:, :, None] * xh
        yz = y.reshape(bsz, l, SSM_GROUPS, D_SSM // SSM_GROUPS) * \
            jax.nn.silu(z.astype(f32)).reshape(bsz, l, SSM_GROUPS, D_SSM // SSM_GROUPS)
        yz = yz * lax.rsqrt(jnp.mean(jnp.square(yz), axis=-1, keepdims=True) + RMS_EPS)
        y_ssm = yz.reshape(bsz, l, D_SSM).astype(x.dtype) * ssm_norm_g[i]

        glu = glu + b_glu[i]
        u = glu[..., :D_CONF] * jax.nn.sigmoid(glu[..., D_CONF:])
        u = causal_depthwise_conv(u, conf_conv_w[i], conf_conv_b[i])
        u = jax.nn.silu(layer_norm(u, conf_ln_g[i], conf_ln_b[i]))
        y_conf = u * jax.nn.silu(cgate)

        mix = jnp.concatenate([y_ssm, y_conf], axis=-1)
        out = jnp.einsum('ble,ed->bld', mix, w_out[i]) + b_out[i]
        h = layer_norm(DEEPNORM_ALPHA * h + out, ln1_g[i], ln1_b[i])

        gate = jax.nn.sigmoid(jnp.einsum('bld,de->ble', h, w_ple_gate[i]))
        ple = jnp.einsum('blq,qd->bld', p[i], w_ple_proj[i])
        h = layer_norm(DEEPNORM_ALPHA * h + gate * ple, ln2_g[i], ln2_b[i])
    return h
```

```python
import numpy as np
from contextlib import ExitStack
import concourse.bass as bass
import concourse.mybir as mybir
from concourse.bass_utils import run_bass_kernel_spmd

F32 = mybir.dt.float32
BF16 = mybir.dt.bfloat16
I32 = mybir.dt.int32
AF = mybir.ActivationFunctionType
ALU = mybir.AluOpType

D_MODEL = 1024
SEQ = 4096
BATCH = 4
D_PLE = 256
NE = 5648
N_SSD = 2576
N_CONF = 3072
ALPHA = 2.0 ** 0.25
LN_EPS = 1e-5
RMS_EPS = 1e-5
NCH_FULL = 16


class Tl:
    def __init__(self, ap, name):
        self.ap = ap
        self.name = name
        self.last_w = None
        self.readers = []
        self.dma_sem = None
        self.dma_cnt = 0
        self.is_psum = False
        self.alias_preds = set()
        self.addr = None
        self.nbytes = 0

    def __getitem__(self, k):
        return self.ap[k]


import os as _os
NO_SCHED = bool(int(_os.environ.get("KNOSCHED", "0")))
SCHED_PRIO = _os.environ.get("KPRIO", "bl")
SCHED_WIN = float(_os.environ.get("KWIN", "0.5"))
SCHED_TRIES = int(_os.environ.get("KTRIES", "6"))
SERIALIZE_PSUM_READS = bool(int(_os.environ.get("KPSUMSER", "1")))
ALIAS_DEPS = bool(int(_os.environ.get("KALIAS", "1")))


class _Cap:
    def __init__(self):
        self.rec = None

    def __getattr__(self, name):
        def f(*args, **kw):
            self.rec = (name, args, kw)
            return self
        return f

    def then_inc(self, *a, **k):
        return self


def _free_elems(ap):
    n = 1
    for d in ap.shape[1:]:
        n *= int(d)
    return n


def _est_cost(eng, fn):
    cap = _Cap()
    try:
        fn(cap)
    except Exception:
        return 0.3
    if cap.rec is None:
        return 0.3
    name, args, kw = cap.rec
    out = kw.get("out", args[0] if args else None)
    try:
        n = _free_elems(out)
    except Exception:
        n = 128
    if eng == "pe":
        if name == "transpose":
            return 0.10
        lhsT = kw.get("lhsT")
        mult = 4.0 if (lhsT is not None and lhsT.dtype == F32) else 1.0
        return mult * max(0.045, n / 2400.0 + 0.01)
    if eng == "act":
        return max(0.45, 0.1 + n * 0.00118)
    if eng == "dve":
        return 0.28 + n / 960.0
    if eng == "pool":
        if name == "tensor_copy":
            return 0.40 + n * 0.0032
        if name == "memset":
            return 0.30 + n * 0.0006
        return 0.40 + n * 0.0025
    return 0.3


class _Op:
    __slots__ = ("eng", "fns", "reads", "writes", "cost", "preds", "kind", "tile", "lat", "idx", "tok",
                 "succs", "npred", "start", "finish", "dma_args", "label")

    def __init__(self, eng, kind):
        self.eng = eng
        self.kind = kind
        self.fns = []
        self.reads = []
        self.writes = []
        self.cost = 0.0
        self.preds = set()
        self.tile = None
        self.lat = 0.0
        self.tok = None
        self.dma_args = None


class FW:
    ENGS = ("pe", "act", "dve", "pool", "sp")

    def __init__(self, nc, stack):
        self.nc = nc
        self.stack = stack
        self.emap = {"pe": nc.tensor, "act": nc.scalar, "dve": nc.vector, "pool": nc.gpsimd, "sp": nc.sync}
        self.cnt = {e: 0 for e in self.ENGS}
        self.sems = {}
        for e in self.ENGS:
            self.sems[e] = stack.enter_context(nc.semaphore("s_" + e))
        self.waited = {e: {} for e in self.ENGS}
        self.ntile = 0
        self.tiles = []
        self.sb_tiles = []
        self.dma_tiles = []
        self.ops = []
        self.pend = None
        self.sim_time = 0.0
        self.log = None
        self.idle = {}
        self.cur_label = None

    def sb(self, stack, shape, dt, name, side=None):
        self.ntile += 1
        nm = f"{name}_{self.ntile}"
        if side is None:
            t = stack.enter_context(self.nc.sbuf_tensor(nm, list(shape), dt))
        else:
            t = stack.enter_context(self.nc.sbuf_tensor(nm, list(shape), dt, side=side))
        tl = Tl(t, nm)
        try:
            ml = self.nc.lookup_mloc(t)
            tl.addr = int(ml.addr)
            tl.nbytes = int(ml.dims[1])
        except Exception:
            tl.addr = None
            tl.nbytes = 0
        if ALIAS_DEPS and tl.addr is not None:
            inh = set()
            for o in self.sb_tiles:
                if o.addr is None:
                    continue
                if o.addr < tl.addr + tl.nbytes and tl.addr < o.addr + o.nbytes:
                    if o.last_w is not None:
                        inh.add(o.last_w)
                    inh.update(o.readers)
                    inh.update(o.alias_preds)
            tl.alias_preds = inh
        self.sb_tiles.append(tl)
        self.tiles.append(tl)
        return tl

    def ps(self, stack, shape, dt, name):
        self.ntile += 1
        nm = f"{name}_{self.ntile}"
        t = stack.enter_context(self.nc.psum_tensor(nm, list(shape), dt))
        tl = Tl(t, nm)
        tl.is_psum = True
        self.tiles.append(tl)
        return tl

    def _deps(self, op):
        for t in list(op.reads) + list(op.writes):
            if t.alias_preds:
                op.preds.update(t.alias_preds)
        for t in op.writes:
            if t.alias_preds and not (op.kind == "ld"):
                pass
        for t in op.reads:
            if t.last_w is not None:
                op.preds.add(t.last_w)
            if t.is_psum and SERIALIZE_PSUM_READS:
                for r in t.readers:
                    if self.ops[r].eng != op.eng:
                        op.preds.add(r)
        for t in op.writes:
            if t.last_w is not None:
                op.preds.add(t.last_w)
            op.preds.update(t.readers)

    def _commit(self, op):
        op.label = self.cur_label
        op.idx = len(self.ops)
        op.preds.discard(op.idx)
        self.ops.append(op)
        for t in op.reads:
            if t not in op.writes:
                t.readers.append(op.idx)
        for t in op.writes:
            t.last_w = op.idx
            t.readers = []
            if t.alias_preds and op.kind != "ld":
                t.alias_preds = set()

    COST_SCALE = {k: float(v) for k, v in (kv.split("=") for kv in _os.environ.get("KCOST", "").split(",") if kv)}

    def op(self, eng, fn, reads=(), writes=(), signal=True):
        if self.pend is not None and self.pend.eng != eng:
            raise RuntimeError("unsignaled group must be closed on the same engine")
        if self.pend is None:
            o = _Op(eng, "c")
        else:
            o = self.pend
        o.fns.append(fn)
        for t in reads:
            if t not in o.reads:
                o.reads.append(t)
        for t in writes:
            if t not in o.writes:
                o.writes.append(t)
        o.cost += _est_cost(eng, fn) * self.COST_SCALE.get(eng, 1.0)
        if not signal:
            self.pend = o
            return None
        self.pend = None
        self._deps(o)
        self._commit(o)
        return o

    def _dsem(self, t):
        if t.dma_sem is None:
            key = "dma_" + t.name
            self.sems[key] = self.stack.enter_context(self.nc.semaphore(key))
            t.dma_sem = key
            self.dma_tiles.append(t)

    def load(self, eng, t, out_ap, in_ap, after=()):
        self._dsem(t)
        o = _Op(eng, "ld")
        o.preds.update(p.idx for p in after if p is not None)
        o.tile = t
        o.writes = [t]
        o.dma_args = (out_ap, in_ap)
        try:
            nbytes = _free_elems(out_ap) * 4
        except Exception:
            nbytes = 4096
        o.cost = 0.08 if eng == "sp" else 0.6
        o.lat = 2.0 + nbytes * 128 / 150e3
        prev = t.last_w
        if False and eng == "sp" and prev is not None and self.ops[prev].kind == "ld" and not t.readers:
            o.preds = set(self.ops[prev].preds)
        else:
            self._deps(o)
        self._commit(o)
        return o

    def store(self, eng, t, out_ap, in_ap):
        self._dsem(t)
        o = _Op(eng, "st")
        o.tile = t
        o.reads = [t]
        o.dma_args = (out_ap, in_ap)
        o.cost = 0.08
        o.lat = 3.0
        self._deps(o)
        self._commit(o)
        return o

    def wait_all(self, eng, ops):
        o = _Op(eng, "w")
        o.preds = set(p.idx for p in ops if p is not None)
        o.cost = 0.01
        self._commit(o)
        return o

    def _schedule(self, win=None, noise=0.0, seed=0, quiet=False):
        ops = self.ops
        win = SCHED_WIN if win is None else win
        import random as _rnd
        rng = _rnd.Random(seed)
        n = len(ops)
        if NO_SCHED:
            order = {e: [] for e in self.ENGS}
            for o in ops:
                order[o.eng].append(o.idx)
            return order, list(range(n))
        for o in ops:
            o.succs = []
            o.npred = len(o.preds)
            o.start = None
            o.finish = None
        for o in ops:
            for p in o.preds:
                ops[p].succs.append(o.idx)
        bl = [0.0] * n
        for o in reversed(ops):
            m = 0.0
            for s_ in o.succs:
                v = bl[s_] + (0.06 if ops[s_].eng == o.eng else 0.25)
                if v > m:
                    m = v
            bl[o.idx] = m + o.cost + o.lat
        blp = [v * (1.0 + noise * (rng.random() - 0.5)) for v in bl] if noise > 0 else bl
        ready = {e: [] for e in self.ENGS}
        rtime = [0.0] * n
        for o in ops:
            if o.npred == 0:
                ready[o.eng].append(o.idx)
        free = {e: 0.0 for e in self.ENGS}
        order = {e: [] for e in self.ENGS}
        glob = []
        done = 0
        while done < n:
            best_e = None; best_t = None
            for e in self.ENGS:
                rl = ready[e]
                if not rl:
                    continue
                tmin = min(rtime[i] for i in rl)
                t_ = tmin if tmin > free[e] else free[e]
                if best_t is None or t_ < best_t:
                    best_t = t_; best_e = e
            e = best_e
            now = best_t
            cands = [i for i in ready[e] if rtime[i] <= now + win]
            if SCHED_PRIO == "bl":
                i = max(cands, key=lambda j: (blp[j], -j))
            else:
                i = min(cands, key=lambda j: (max(rtime[j], free[e]), j))
            stt = rtime[i] if rtime[i] > free[e] else free[e]
            ready[e].remove(i)
            o = ops[i]
            o.start = stt
            fin = stt + o.cost
            free[e] = fin
            o.finish = fin + o.lat
            order[e].append(i)
            glob.append(i)
            done += 1
            for s_ in o.succs:
                so = ops[s_]
                lat = 0.06 if so.eng == e else 0.25
                t2 = o.finish + lat
                if t2 > rtime[s_]:
                    rtime[s_] = t2
                so.npred -= 1
                if so.npred == 0:
                    ready[so.eng].append(s_)
        mk = max([0.0] + [o.finish for o in ops])
        self.last_mk = mk
        if quiet:
            return order, glob
        if self.log is not None and n > 1000:
            import collections as _c
            sp = _c.defaultdict(lambda: [1e18, 0.0])
            for o in ops:
                if o.label is None:
                    continue
                d = sp[o.label]
                d[0] = min(d[0], o.start); d[1] = max(d[1], o.finish)
            agg = _c.defaultdict(list)
            for (sname, ch), (a_, b_) in sp.items():
                agg[sname].append(b_ - a_)
            print("   stage spans:", ", ".join("%s=%.1f(max %.1f)" % (k, sum(v) / len(v), max(v)) for k, v in sorted(agg.items())))
        if self.log is not None and n > 1000:
            cp = [0.0] * n; par = [-1] * n
            for o in ops:
                best = 0.0; bp = -1
                for p in o.preds:
                    v = cp[p] + (0.06 if ops[p].eng == o.eng else 0.25)
                    if v > best:
                        best = v; bp = p
                cp[o.idx] = best + o.cost + o.lat; par[o.idx] = bp
            end = max(range(n), key=lambda i: cp[i])
            print("   critical path length %.1f" % cp[end])
            chain = []
            i = end
            while i >= 0:
                chain.append(i); i = par[i]
            chain.reverse()
            import collections
            agg = collections.OrderedDict()
            for i in chain:
                o = ops[i]
                key = (o.eng, (o.writes[0].name.rsplit("_", 1)[0] if o.writes else o.kind))
                agg[key] = agg.get(key, 0.0) + o.cost + o.lat
            top = sorted(agg.items(), key=lambda kv: -kv[1])[:12]
            print("   cp composition:", ", ".join("%s/%s=%.0f" % (k[0], k[1], v) for k, v in top))
        if self.log is not None:
            top = sorted([kv for kv in self.idle.items() if kv[0][0] in ('pe', 'dve', 'act')], key=lambda kv: -kv[1])[:16]
            for k, v in top:
                print("   idle %-5s waits %-5s %-10s -> %-10s %.1f" % (k[0], k[1], k[2], k[3], v))
            self.idle = {}
            busy = {e: sum(ops[i].cost for i in order[e]) for e in self.ENGS}
            print("EPOCH n=%d makespan=%.1f " % (n, mk) + " ".join("%s=%.0f" % (e, busy[e]) for e in self.ENGS))
        return order, glob

    def _emit_waits(self, eng, toks):
        need = {}
        for k, v in toks:
            if k == eng and eng == "pe":
                continue
            if self.waited[eng].get(k, 0) >= v:
                continue
            if need.get(k, 0) < v:
                need[k] = v
        e = self.emap[eng]
        for k, v in need.items():
            self.waited[eng][k] = v
            e.wait_ge(self.sems[k], v)
            if self.log is not None:
                self.log[eng].append(("w", k, v))

    def flush(self):
        if self.pend is not None:
            raise RuntimeError("open unsignaled group at flush")
        ops = self.ops
        if not ops:
            return
        if NO_SCHED or len(ops) < 200 or SCHED_TRIES <= 1:
            order, glob = self._schedule()
            best_mk = self.last_mk
        else:
            best = None
            cfgs = [(0.5, 0.0, 0), (0.0, 0.0, 0), (0.25, 0.0, 0), (1.0, 0.0, 0)]
            for s_ in range(SCHED_TRIES - len(cfgs)):
                cfgs.append((0.5 if s_ % 2 == 0 else 0.25, 0.04 + 0.02 * (s_ % 3), s_ + 1))
            for (w_, nz_, sd_) in cfgs:
                o_, g_ = self._schedule(win=w_, noise=nz_, seed=sd_, quiet=True)
                if best is None or self.last_mk < best[0]:
                    best = (self.last_mk, o_, g_, (w_, nz_, sd_))
            best_mk, order, glob, cfg = best
            if self.log is not None:
                print("   best schedule cfg", cfg, "makespan %.1f" % best_mk)
        self.sim_time += best_mk
        for e in self.ENGS:
            for i in order[e]:
                o = ops[i]
                if o.kind == "c":
                    self.cnt[e] += 1
                    o.tok = (e, self.cnt[e])
        for i in glob:
            o = ops[i]
            if o.kind in ("ld", "st"):
                o.tile.dma_cnt += 16
                o.tok = (o.tile.dma_sem, o.tile.dma_cnt)
        for e in self.ENGS:
            eng = self.emap[e]
            for i in order[e]:
                o = ops[i]
                toks = [ops[p].tok for p in o.preds if ops[p].tok is not None]
                self._emit_waits(e, toks)
                if o.kind == "c":
                    ins = None
                    for fn in o.fns:
                        ins = fn(eng)
                    ins.then_inc(self.sems[e], 1)
                    if self.log is not None:
                        self.log[e].append(("i", e, 1))
                elif o.kind in ("ld", "st"):
                    out_ap, in_ap = o.dma_args
                    eng.dma_start(out=out_ap, in_=in_ap).then_inc(self.sems[o.tile.dma_sem], 16)
                    if self.log is not None:
                        self.log[e].append(("i", o.tile.dma_sem, 16))
        self.ops = []
        for t in self.tiles:
            t.last_w = None
            t.readers = []
            t.alias_preds = set()

    def barrier(self):
        self.flush()
        toks = [(e, self.cnt[e]) for e in ("pe", "act", "dve", "pool") if self.cnt[e] > 0]
        for t in self.dma_tiles:
            if t.dma_cnt:
                toks.append((t.dma_sem, t.dma_cnt))
        for e in self.ENGS:
            self._emit_waits(e, toks)


import os as _os2
NEWTON_ENG = _os2.environ.get("KNEWTON", "pool")
NEWTON_CUR = [NEWTON_ENG]
NEWTON_A = _os2.environ.get("KNEWTONA", "pool")
NEWTON_B = _os2.environ.get("KNEWTONB", "pool")
EVAC_BC = _os2.environ.get("KEVAC", "act")
C_TT = _os2.environ.get("KCTT", "dve")
MT_ENG = _os2.environ.get("KMT", "pool")
BC_HILO = bool(int(_os2.environ.get("KBCHILO", "1")))
NSQ = int(_os2.environ.get("KNSQ", "4"))
C_B2 = _os2.environ.get("KCB2", "pool")
W_INFLIGHT = int(_os2.environ.get("KWINF", "3"))
PASS_BARRIER = bool(int(_os2.environ.get("KPASSBAR", "0")))
LNC_ENGS = _os2.environ.get("KLNC", "dve,pool,act").split(",")
C_PB = _os2.environ.get("KCPB", "act")


def pipeline(stage_fns, chunks):
    n = len(chunks)
    ns = len(stage_fns)
    for t in range(n + ns - 1):
        for s_ in reversed(range(ns)):
            i = t - s_
            if 0 <= i < n:
                stage_fns[s_](chunks[i], i)


def build(NCH=NCH_FULL, debug=False):
    nc = bass.Bass("TRN2", target_bir_lowering=False)
    NT = 2 * NCH * 128
    NM = NCH * 128

    def din(name, shape):
        return nc.dram_tensor(name, list(shape), F32, kind="ExternalInput").ap()

    xa = din("xa", [NT, D_MODEL])
    pa = din("pa", [NM, D_PLE])
    flag_d = din("flag", [128, 1])
    w_in = din("w_in", [D_MODEL, NE])
    w_out = din("w_out", [2048, D_MODEL])
    w_gate = din("w_gate", [D_MODEL, D_MODEL])
    w_ple = din("w_ple", [D_PLE, D_MODEL])
    g0fm_d = din("g0fm", [128, 8]); b0fm_d = din("b0fm", [128, 8])
    g0bc_d = din("g0bc", [128, 1024])
    b0row_d = din("b0row", [1, 1024]); boutrow_d = din("boutrow", [1, 1024])
    c4w_d = din("c4w", [128, 12, 4]); c4b_d = din("c4b", [128, 12])
    dtb_d = din("dtb", [128, 16]); alog_d = din("alog", [128, 16]); dfm_d = din("dfm", [128, 8])
    ngfm_d = din("ngfm", [128, 8])
    bglu_d = din("bglu", [1, 2048])
    c31w_d = din("c31w", [128, 8, 31]); c31b_d = din("c31b", [128, 8])
    clg_d = din("clg", [128, 8]); clb_d = din("clb", [128, 8])
    g1fm_d = din("g1fm", [128, 8]); b1fm_d = din("b1fm", [128, 8])
    g1bc_d = din("g1bc", [128, 1024]); b1bc_d = din("b1bc", [128, 1024])
    g2bc_d = din("g2bc", [128, 1024]); b2bc_d = din("b2bc", [128, 1024])
    out_d = nc.dram_tensor("out", [NM, D_MODEL], F32, kind="ExternalOutput").ap()
    if debug:
        dbg_ys = nc.dram_tensor("dbg_ys", [128, 8 * NM], F32, kind="ExternalOutput").ap()
        dbg_yc = nc.dram_tensor("dbg_yc", [128, 8 * NM], F32, kind="ExternalOutput").ap()

    with ExitStack() as st:
        fw = FW(nc, st)
        PS = [fw.ps(st, [128, 512], F32, f"bank{i}") for i in range(8)]

        def psb(i):
            return PS[i].ap[:, :].bitcast(BF16)

        R = "right"
        identf = fw.sb(st, [128, 128], F32, "identf", R)
        identb = fw.sb(st, [128, 128], BF16, "identb", R)
        onesf = fw.sb(st, [128, 128], F32, "onesf", R)
        onesb = fw.sb(st, [128, 128], BF16, "onesb", R)
        ones33 = fw.sb(st, [33, 128], BF16, "ones33", R)
        flag = fw.sb(st, [128, 1], F32, "flag", R)
        g0fm = fw.sb(st, [128, 8], F32, "g0fm", R)
        b0fm = fw.sb(st, [128, 8], F32, "b0fm", R)
        YC = fw.sb(st, [128, 8, NM], BF16, "YC", R)

        fw.op("pool", lambda e: e.memset(identf[:, :], 0.0), writes=[identf])
        fw.op("pool", lambda e: e.affine_select(out=identf[:, :], in_=identf[:, :], pattern=[[-1, 128]],
                                                compare_op=ALU.not_equal, fill=1.0, base=0, channel_multiplier=1),
              reads=[identf], writes=[identf])
        fw.op("pool", lambda e: e.tensor_copy(out=identb[:, :], in_=identf[:, :]), reads=[identf], writes=[identb])
        fw.op("pool", lambda e: e.memset(onesf[:, :], 1.0), writes=[onesf])
        fw.op("pool", lambda e: e.memset(onesb[:, :], 1.0), writes=[onesb])
        fw.op("pool", lambda e: e.memset(ones33[:, :], 1.0), writes=[ones33])
        fw.load("sp", flag, flag[:, :], flag_d[:, :])
        fw.load("sp", g0fm, g0fm[:, :], g0fm_d[:, :])
        fw.load("sp", b0fm, b0fm[:, :], b0fm_d[:, :])

        def rsqrt_chain(tiles, ve, n, iters=2, neng=None):
            neng = neng or NEWTON_CUR[0]
            ti, y, tb = tiles
            fw.op("dve", lambda e: e.tensor_single_scalar(out=ti[:, 0:n], in_=ve[:, 0:n].bitcast(I32), scalar=1,
                                                          op=ALU.arith_shift_right), reads=[ve], writes=[ti])
            fw.op("dve", lambda e: e.tensor_scalar(out=y[:, 0:n].bitcast(I32), in0=ti[:, 0:n], scalar1=-1.0,
                                                   scalar2=1597463007.0, op0=ALU.mult, op1=ALU.add),
                  reads=[ti], writes=[y])
            if neng == "act" and n == 1:
                nh = ti
                fw.op("act", lambda e: e.activation(out=nh[:, 0:1].bitcast(F32), in_=ve[:, 0:1], func=AF.Copy, scale=-0.5),
                      reads=[ve, ti], writes=[nh])
                for _ in range(iters):
                    fw.op("act", lambda e: e.activation(out=tb[:, 0:1], in_=y[:, 0:1], func=AF.Square), reads=[y], writes=[tb])
                    fw.op("act", lambda e: e.activation(out=tb[:, 0:1], in_=tb[:, 0:1], func=AF.Identity,
                                                        scale=nh[:, 0:1].bitcast(F32), bias=1.5), reads=[tb, nh], writes=[tb])
                    fw.op("act", lambda e: e.activation(out=y[:, 0:1], in_=y[:, 0:1], func=AF.Copy, scale=tb[:, 0:1]),
                          reads=[y, tb], writes=[y])
                return y
            for _ in range(iters):
                fw.op(neng, lambda e: e.tensor_tensor(out=tb[:, 0:n], in0=y[:, 0:n], in1=y[:, 0:n], op=ALU.mult),
                      reads=[y], writes=[tb])
                fw.op(neng, lambda e: e.tensor_tensor(out=tb[:, 0:n], in0=tb[:, 0:n], in1=ve[:, 0:n], op=ALU.mult),
                      reads=[tb, ve], writes=[tb])
                fw.op(neng, lambda e: e.tensor_scalar(out=tb[:, 0:n], in0=tb[:, 0:n], scalar1=-0.5, scalar2=1.5,
                                                      op0=ALU.mult, op1=ALU.add), reads=[tb], writes=[tb])
                fw.op(neng, lambda e: e.tensor_tensor(out=y[:, 0:n], in0=y[:, 0:n], in1=tb[:, 0:n], op=ALU.mult),
                      reads=[y, tb], writes=[y])
            return y

        class LNS:
            def __init__(self, stk, name, nsets=4):
                self.sets = []
                for i in range(nsets):
                    self.sets.append(dict(
                        stats=fw.sb(stk, [128, 2, 6], F32, name + "st"),
                        mv=fw.sb(stk, [128, 2], F32, name + "mv"),
                        ve=fw.sb(stk, [128, 1], F32, name + "ve"),
                        ti=fw.sb(stk, [128, 1], I32, name + "ti"),
                        y=fw.sb(stk, [128, 1], F32, name + "y"),
                        tb=fw.sb(stk, [128, 1], F32, name + "tb"),
                        nmr=fw.sb(stk, [128, 1], F32, name + "nmr")))
                self.i = 0

            def run(self, xt):
                s = self.sets[self.i % len(self.sets)]
                self.i += 1
                for c in range(2):
                    fw.op("dve", lambda e, c=c: e.bn_stats(out=s["stats"][:, c, :], in_=xt[:, c * 512:(c + 1) * 512]),
                          reads=[xt], writes=[s["stats"]])
                fw.op("dve", lambda e: e.bn_aggr(out=s["mv"][:, :], in_=s["stats"][:, :, :]),
                      reads=[s["stats"]], writes=[s["mv"]])
                fw.op("dve", lambda e: e.tensor_scalar_add(out=s["ve"][:, :], in0=s["mv"][:, 1:2], scalar1=LN_EPS),
                      reads=[s["mv"]], writes=[s["ve"]])
                engs = getattr(self, "engs", None)
                ne = engs[(self.i - 1) % len(engs)] if engs else NEWTON_CUR[0]
                y = rsqrt_chain((s["ti"], s["y"], s["tb"]), s["ve"], 1, neng=ne)
                if ne == "act":
                    fw.op("act", lambda e: e.activation(out=s["nmr"][:, :], in_=s["mv"][:, 0:1], func=AF.Copy, scale=y[:, 0:1]),
                          reads=[s["mv"], y], writes=[s["nmr"]])
                    fw.op("act", lambda e: e.activation(out=s["nmr"][:, :], in_=s["nmr"][:, :], func=AF.Copy, scale=-1.0),
                          reads=[s["nmr"]], writes=[s["nmr"]])
                elif ne == "dve":
                    fw.op("dve", lambda e: e.scalar_tensor_tensor(out=s["nmr"][:, :], in0=s["mv"][:, 0:1], scalar=-1.0,
                                                                   in1=y[:, :], op0=ALU.mult, op1=ALU.mult),
                          reads=[s["mv"], y], writes=[s["nmr"]])
                else:
                    fw.op(ne, lambda e: e.tensor_tensor(out=s["nmr"][:, :], in0=s["mv"][:, 0:1], in1=y[:, :], op=ALU.mult),
                          reads=[s["mv"], y], writes=[s["nmr"]])
                    fw.op(ne, lambda e: e.tensor_scalar_mul(out=s["nmr"][:, :], in0=s["nmr"][:, :], scalar1=-1.0),
                          reads=[s["nmr"]], writes=[s["nmr"]])
                return y, s["nmr"]

        ptc = [0]

        def next_pt():
            i = 2 + (ptc[0] % 2)
            ptc[0] += 1
            return i

        accc = [0]

        def next_acc():
            i = accc[0] % 2
            accc[0] += 1
            return i

        def front_end(tiles, lns, row0):
            xin, xnb, hT = tiles
            fw.load("sp", xin, xin[:, :], xa[row0:row0 + 128, :])
            rstd, nmr = lns.run(xin)
            fw.op("act", lambda e: e.activation(out=xnb[:, :], in_=xin[:, :], func=AF.Identity, bias=nmr[:, :],
                                                scale=rstd[:, :]), reads=[xin, nmr, rstd], writes=[xnb])
            pt = next_pt()
            for k in range(8):
                fw.op("pe", lambda e, k=k: e.transpose(out=psb(pt)[:, k * 128:(k + 1) * 128],
                                                       in_=xnb[:, k * 128:(k + 1) * 128], identity=identb[:, :]),
                      reads=[xnb, identb], writes=[PS[pt]], signal=(k == 7))
            evac_T(pt, hT, g0fm, b0fm)
            return hT

        def hsl(hT, k):
            return hT[:, k, :]

        evac_eng = ["dve"]

        def evac_T(pt, hT, gfm, bfm):
            for k in range(8):
                if evac_eng[0] == "act":
                    fw.op("act", lambda e, k=k: e.activation(out=hsl(hT, k), in_=psb(pt)[:, k * 128:(k + 1) * 128],
                                                             func=AF.Identity, scale=gfm[:, k:k + 1], bias=bfm[:, k:k + 1]),
                          reads=[PS[pt], gfm, bfm], writes=[hT])
                else:
                    fw.op("dve", lambda e, k=k: e.tensor_scalar(out=hsl(hT, k), in0=psb(pt)[:, k * 128:(k + 1) * 128],
                                                                scalar1=gfm[:, k:k + 1], scalar2=bfm[:, k:k + 1],
                                                                op0=ALU.mult, op1=ALU.add),
                          reads=[PS[pt], gfm, bfm], writes=[hT])

        def mk_hT(stk, name):
            return fw.sb(stk, [128, 8, 128], BF16, name)

        class WBlocks:
            def __init__(self, stk, name, src, K, blocks, inflight=W_INFLIGHT):
                self.blocks = []
                self.tiles = []
                ops_ = []
                for j, (c0, w) in enumerate(blocks):
                    t_ = fw.sb(stk, [128, K, w], BF16, name)
                    after = [ops_[j - inflight]] if j >= inflight else []
                    o_ = fw.load("pool", t_, t_[:, :, :], src[:, c0:c0 + w].rearrange("(k p) n -> p k n", p=128), after=after)
                    ops_.append(o_)
                    self.blocks.append((c0, w, t_))
                    self.tiles.append(t_)

            def get(self, k, col0, width):
                for (c0, w, t_) in self.blocks:
                    if c0 <= col0 and col0 + width <= c0 + w:
                        return t_, t_[:, k, col0 - c0:col0 - c0 + width]
                raise KeyError((col0, width))

        def proj(hT, W, col0, width, bank, pre=None):
            first = True
            if pre is not None:
                for (l_t, l_ap, r_t, r_ap) in pre:
                    fw.op("pe", lambda e, l_ap=l_ap, r_ap=r_ap, first=first: e.matmul(
                        PS[bank][:, 0:width], lhsT=l_ap, rhs=r_ap, start=first, stop=False),
                        reads=[l_t, r_t], writes=[PS[bank]], signal=False)
                    first = False
            for k in range(8):
                fw.op("pe", lambda e, k=k, first=first: e.matmul(
                    PS[bank][:, 0:width], lhsT=hsl(hT, k), rhs=W.get(k, col0, width)[1],
                    start=(first and k == 0), stop=(k == 7)),
                    reads=[hT, W.get(k, col0, width)[0]], writes=[PS[bank]], signal=(k == 7))

        def hilo_rows(stk, dst, src_d_list, ncols, CH=1024):
            f1 = fw.sb(stk, [33, CH], F32, "hl_f1")
            f2 = fw.sb(stk, [33, CH], F32, "hl_f2") if len(src_d_list) > 1 else None
            hf = fw.sb(stk, [33, CH], F32, "hl_hf")
            for c0 in range(0, ncols, CH):
                for r_ in (0, 32):
                    fw.load("sp", f1, f1[r_:r_ + 1, :], src_d_list[0][1][:, c0:c0 + CH])
                    if len(src_d_list) > 1:
                        fw.load("sp", f2, f2[r_:r_ + 1, :], src_d_list[1][1][:, c0:c0 + CH])
                if len(src_d_list) > 1:
                    for r_ in (0, 32):
                        fw.op("dve", lambda e, r_=r_: e.scalar_tensor_tensor(
                            out=f1[r_:r_ + 1, :], in0=f1[r_:r_ + 1, :], scalar=src_d_list[0][0], in1=f2[r_:r_ + 1, :],
                            op0=ALU.mult, op1=ALU.add), reads=[f1, f2], writes=[f1])
                fw.op("dve", lambda e, c0=c0: e.tensor_copy(out=dst[0:1, c0:c0 + CH], in_=f1[0:1, :]), reads=[f1, dst], writes=[dst])
                fw.op("dve", lambda e, c0=c0: e.tensor_copy(out=dst[32:33, c0:c0 + CH], in_=f1[32:33, :]), reads=[f1, dst], writes=[dst])
                fw.op("dve", lambda e, c0=c0: e.tensor_copy(out=hf[32:33, :], in_=dst[32:33, c0:c0 + CH]), reads=[dst], writes=[hf])
                fw.op("dve", lambda e, c0=c0: e.tensor_tensor(out=dst[32:33, c0:c0 + CH], in0=f1[32:33, :], in1=hf[32:33, :],
                                                              op=ALU.subtract), reads=[f1, hf, dst], writes=[dst])

        with ExitStack() as sbk:
            Wc = WBlocks(sbk, "Wc", w_in[:, N_SSD:NE], 8,
                         [(1024, 512), (0, 512), (1536, 512), (512, 512), (2048, 512), (2560, 512)])
            diag31 = fw.sb(sbk, [128, 8, 31, 128], BF16, "diag31")
            clg = fw.sb(sbk, [128, 8], F32, "clg")
            clb = fw.sb(sbk, [128, 8], F32, "clb")
            hclg = fw.sb(sbk, [128, 8], F32, "hclg")
            hclb = fw.sb(sbk, [128, 8], F32, "hclb")
            c31b = fw.sb(sbk, [128, 8], F32, "c31b")
            bglu2 = fw.sb(sbk, [33, 2048], BF16, "bglu2")
            fw.load("sp", clg, clg[:, :], clg_d[:, :])
            fw.load("sp", clb, clb[:, :], clb_d[:, :])
            fw.load("sp", c31b, c31b[:, :], c31b_d[:, :])
            fw.op("pool", lambda e: e.memset(bglu2[:, :], 0.0), writes=[bglu2])
            c31w = fw.sb(sbk, [128, 8, 31], F32, "c31w")
            w31s = fw.sb(sbk, [128, 8, 31], F32, "w31s")
            fw.load("sp", c31w, c31w[:, :, :], c31w_d[:, :, :])
            fw.op("dve", lambda e: e.tensor_scalar_mul(out=w31s[:, :, :], in0=c31w[:, :, :], scalar1=0.5),
                  reads=[c31w], writes=[w31s])
            for ct in range(8):
                eng = "pool" if ct in (0, 4) else "dve"
                fw.op(eng, lambda e, ct=ct: e.tensor_tensor(
                    out=diag31[:, ct, :, :],
                    in0=identf[:, :].unsqueeze(1).to_broadcast([128, 31, 128]),
                    in1=w31s[:, ct, :].unsqueeze(2).to_broadcast([128, 31, 128]),
                    op=ALU.mult), reads=[identf, w31s], writes=[diag31])
            hilo_rows(sbk, bglu2, [(1.0, bglu_d)], 2048)
            fw.op("dve", lambda e: e.tensor_scalar_mul(out=hclg[:, :], in0=clg[:, :], scalar1=0.5), reads=[clg], writes=[hclg])
            fw.op("dve", lambda e: e.tensor_scalar_mul(out=hclb[:, :], in0=clb[:, :], scalar1=0.5), reads=[clb], writes=[hclb])

            lnsB = LNS(sbk, "lnB")
            xins = [fw.sb(sbk, [128, 1024], F32, "xinB") for _ in range(2)]
            xnbs = [fw.sb(sbk, [128, 1024], BF16, "xnbB") for _ in range(2)]
            hTs = [mk_hT(sbk, "hTB") for _ in range(2)]
            tgs = [fw.sb(sbk, [128, 512], F32, "tg") for _ in range(2)]
            u2s = [fw.sb(sbk, [128, 1024], BF16, "u2") for _ in range(2)]
            _gcg = fw.sb(sbk, [128, 1024], BF16, "gcg")
            gcgs = [_gcg, _gcg]
            u_fms = [fw.sb(sbk, [128, 8, 158], BF16, "u_fm") for _ in range(2)]
            gcg_fms = [fw.sb(sbk, [128, 8, 128], BF16, "gcg_fm") for _ in range(2)]
            cvs = [fw.sb(sbk, [128, 8, 128], F32, "cv") for _ in range(2)]
            sqs = [fw.sb(sbk, [128, 128], F32, "sqs") for _ in range(NSQ)]
            _mean = fw.sb(sbk, [128, 128], F32, "mean")
            mean_l = [_mean, _mean]
            _vev = fw.sb(sbk, [128, 128], F32, "vev")
            vev_l = [_vev, _vev]
            _tiv = fw.sb(sbk, [128, 128], I32, "tiv")
            tiv_l = [_tiv, _tiv]
            _yv = fw.sb(sbk, [128, 128], F32, "yv")
            yv_l = [_yv, _yv]
            _tbv = fw.sb(sbk, [128, 128], F32, "tbv")
            tbv_l = [_tbv, _tbv]
            for t_ in u_fms:
                fw.op("pool", lambda e, t_=t_: e.memset(t_[:, :, :], 0.0), writes=[t_])
            sqc = [0]

            def next_sq():
                t_ = sqs[sqc[0] % NSQ]
                sqc[0] += 1
                return t_

            def B_s0(m, i):
                c = NCH + m
                front_end((xins[i % 2], xnbs[i % 2], hTs[i % 2]), lnsB, c * 128)

            def B_s1(m, i):
                hT = hTs[i % 2]
                u2 = u2s[i % 2]
                gcg = gcgs[i % 2]
                u_fm = u_fms[i % 2]
                u_nx = u_fms[(i + 1) % 2]
                gcg_fm = gcg_fms[i % 2]
                for blk in range(2):
                    def bias_pre(col):
                        return [(ones33, ones33[:, :], bglu2, bglu2[:, col:col + 512])]
                    bg = next_acc()
                    proj(hT, Wc, 1024 + blk * 512, 512, bg, pre=bias_pre(1024 + blk * 512))
                    tgt = tgs[blk]
                    fw.op("act", lambda e, bg=bg, tgt=tgt: e.activation(out=tgt[:, :], in_=PS[bg][:, :], func=AF.Tanh,
                                                                        scale=0.5), reads=[PS[bg]], writes=[tgt])
                    bv = next_acc()
                    proj(hT, Wc, blk * 512, 512, bv, pre=bias_pre(blk * 512))
                    fw.op("dve", lambda e, bv=bv, tgt=tgt, blk=blk: e.scalar_tensor_tensor(
                        out=u2[:, blk * 512:(blk + 1) * 512], in0=tgt[:, :], scalar=1.0, in1=PS[bv][:, :],
                        op0=ALU.add, op1=ALU.mult), reads=[tgt, PS[bv]], writes=[u2])
                ptu = next_pt()
                for k in range(8):
                    fw.op("pe", lambda e, k=k: e.transpose(out=psb(ptu)[:, k * 128:(k + 1) * 128],
                                                           in_=u2[:, k * 128:(k + 1) * 128], identity=identb[:, :]),
                          reads=[u2, identb], writes=[PS[ptu]], signal=(k == 7))
                fw.op("act", lambda e: e.activation(out=u_fm[:, :, 30:158],
                                                    in_=psb(ptu)[:, :].rearrange("p (a b) -> p a b", a=8), func=AF.Copy),
                      reads=[PS[ptu]], writes=[u_fm])
                fw.op("pool", lambda e: e.tensor_copy(out=u_nx[:, :, 0:30], in_=u_fm[:, :, 128:158]),
                      reads=[u_fm, u_nx], writes=[u_nx])
                if m == -1:
                    fw.op("pool", lambda e: e.tensor_scalar_mul(out=u_nx[:, :, 0:30], in0=u_nx[:, :, 0:30],
                                                                scalar1=flag[:, 0:1]), reads=[u_nx, flag], writes=[u_nx])
                    return
                for blk in range(2):
                    bc_ = next_acc()
                    proj(hT, Wc, 2048 + blk * 512, 512, bc_)
                    tgt = tgs[blk]
                    fw.op("act", lambda e, bc_=bc_, tgt=tgt: e.activation(out=tgt[:, :], in_=PS[bc_][:, :],
                                                                          func=AF.Tanh, scale=0.5),
                          reads=[PS[bc_]], writes=[tgt])
                    fw.op("dve", lambda e, bc_=bc_, tgt=tgt, blk=blk: e.scalar_tensor_tensor(
                        out=gcg[:, blk * 512:(blk + 1) * 512], in0=tgt[:, :], scalar=1.0, in1=PS[bc_][:, :],
                        op0=ALU.add, op1=ALU.mult), reads=[tgt, PS[bc_]], writes=[gcg])
                ptg = next_pt()
                for k in range(8):
                    fw.op("pe", lambda e, k=k: e.transpose(out=psb(ptg)[:, k * 128:(k + 1) * 128],
                                                           in_=gcg[:, k * 128:(k + 1) * 128], identity=identb[:, :]),
                          reads=[gcg, identb], writes=[PS[ptg]], signal=(k == 7))
                fw.op("act", lambda e: e.activation(out=gcg_fm[:, :, :],
                                                    in_=psb(ptg)[:, :].rearrange("p (a b) -> p a b", a=8), func=AF.Copy,
                                                    scale=0.25),
                      reads=[PS[ptg]], writes=[gcg_fm])

            def B_s2(m, i):
                if m == -1:
                    return
                u_fm = u_fms[i % 2]
                cv = cvs[i % 2]
                for half in range(2):
                    bk = 4 + half
                    for j in range(4):
                        ct = half * 4 + j
                        for k in range(31):
                            fw.op("pe", lambda e, j=j, ct=ct, k=k, bk=bk: e.matmul(
                                PS[bk][:, j * 128:(j + 1) * 128], lhsT=diag31[:, ct, k, :], rhs=u_fm[:, ct, k:k + 128],
                                start=(k == 0), stop=(k == 30)), reads=[diag31, u_fm], writes=[PS[bk]],
                                signal=(j == 3 and k == 30))
                    for j in range(4):
                        ct = half * 4 + j
                        if j % 2 == 0:
                            fw.op("act", lambda e, j=j, ct=ct, bk=bk: e.activation(
                                out=cv[:, ct, :], in_=PS[bk][:, j * 128:(j + 1) * 128], func=AF.Identity,
                                bias=c31b[:, ct:ct + 1]), reads=[PS[bk], c31b], writes=[cv])
                        else:
                            fw.op("dve", lambda e, j=j, ct=ct, bk=bk: e.tensor_scalar_add(
                                out=cv[:, ct, :], in0=PS[bk][:, j * 128:(j + 1) * 128], scalar1=c31b[:, ct:ct + 1]),
                                reads=[PS[bk], c31b], writes=[cv])

            def B_s3(m, i):
                if m == -1:
                    return
                cv = cvs[i % 2]
                gcg_fm = gcg_fms[i % 2]
                mean = mean_l[i % 2]; vev = vev_l[i % 2]; tiv = tiv_l[i % 2]; yv = yv_l[i % 2]; tbv = tbv_l[i % 2]
                msq = tbv
                pst = 6
                for ct in range(8):
                    fw.op("pe", lambda e, ct=ct: e.matmul(PS[pst][:, 0:128], lhsT=onesf[:, :], rhs=cv[:, ct, :],
                                                          start=(ct == 0), stop=(ct == 7)),
                          reads=[onesf, cv], writes=[PS[pst]], signal=(ct == 7))
                for ct in range(8):
                    sq_ = next_sq()
                    fw.op("pool", lambda e, ct=ct, sq_=sq_: e.tensor_tensor(out=sq_[:, :], in0=cv[:, ct, :], in1=cv[:, ct, :],
                                                                            op=ALU.mult), reads=[cv], writes=[sq_])
                    fw.op("pe", lambda e, ct=ct, sq_=sq_: e.matmul(PS[pst][:, 128:256], lhsT=onesf[:, :], rhs=sq_[:, :],
                                                                   start=(ct == 0), stop=(ct == 7)),
                          reads=[onesf, sq_], writes=[PS[pst]], signal=True)
                fw.op("dve", lambda e: e.tensor_scalar_mul(out=mean[:, :], in0=PS[pst][:, 0:128], scalar1=1.0 / 1024.0),
                      reads=[PS[pst]], writes=[mean])
                fw.op("dve", lambda e: e.tensor_tensor(out=msq[:, :], in0=mean[:, :], in1=mean[:, :], op=ALU.mult),
                      reads=[mean], writes=[msq])
                fw.op("dve", lambda e: e.scalar_tensor_tensor(out=vev[:, :], in0=PS[pst][:, 128:256], scalar=1.0 / 1024.0,
                                                               in1=msq[:, :], op0=ALU.mult, op1=ALU.subtract),
                      reads=[PS[pst], msq], writes=[vev])
                fw.op("dve", lambda e: e.tensor_scalar(out=vev[:, :], in0=vev[:, :], scalar1=0.0, scalar2=LN_EPS,
                                                       op0=ALU.max, op1=ALU.add), reads=[vev], writes=[vev])
                rv = rsqrt_chain((tiv, yv, tbv), vev, 128)
                fw.op("dve", lambda e: e.tensor_tensor(out=cv[:, :, :], in0=cv[:, :, :],
                                                       in1=mean[:, :].unsqueeze(1).to_broadcast([128, 8, 128]),
                                                       op=ALU.subtract), reads=[cv, mean], writes=[cv])
                fw.op("pool", lambda e: e.tensor_tensor(out=cv[:, :, :], in0=cv[:, :, :],
                                                        in1=rv[:, :].unsqueeze(1).to_broadcast([128, 8, 128]),
                                                        op=ALU.mult), reads=[cv, rv], writes=[cv])
                for ct in range(8):
                    tl_ = next_sq()
                    fw.op("act", lambda e, ct=ct, tl_=tl_: e.activation(out=tl_[:, :], in_=cv[:, ct, :], func=AF.Tanh,
                                                                        scale=hclg[:, ct:ct + 1], bias=hclb[:, ct:ct + 1]),
                          reads=[cv, hclg, hclb], writes=[tl_])
                    fw.op("dve", lambda e, ct=ct: e.tensor_scalar(out=cv[:, ct, :], in0=cv[:, ct, :],
                                                                  scalar1=clg[:, ct:ct + 1], scalar2=clb[:, ct:ct + 1],
                                                                  op0=ALU.mult, op1=ALU.add),
                          reads=[cv, clg, clb], writes=[cv])
                    fw.op("dve", lambda e, ct=ct, tl_=tl_: e.scalar_tensor_tensor(
                        out=cv[:, ct, :], in0=tl_[:, :], scalar=1.0, in1=cv[:, ct, :], op0=ALU.add, op1=ALU.mult),
                        reads=[tl_, cv], writes=[cv])
                fw.op("pool", lambda e, m=m: e.tensor_tensor(out=YC[:, :, m * 128:(m + 1) * 128], in0=cv[:, :, :],
                                                             in1=gcg_fm[:, :, :], op=ALU.mult),
                      reads=[cv, gcg_fm], writes=[YC])

            evac_eng[0] = EVAC_BC
            NEWTON_CUR[0] = NEWTON_B
            pipeline([B_s0, B_s1, B_s2, B_s3], list(range(-1, NCH)))
            if PASS_BARRIER:
                fw.barrier()

        YS = fw.sb(st, [128, 8, NM], BF16, "YS", R)
        with ExitStack() as sa:
            Wssd = WBlocks(sa, "Wssd", w_in[:, 0:N_SSD], 8,
                           [(0, 512), (512, 512), (2048, 528), (1024, 512), (1536, 512)])
            diag4 = fw.sb(sa, [128, 12, 5, 128], BF16, "diag4")
            dtb = fw.sb(sa, [128, 16], F32, "dtb")
            abc = fw.sb(sa, [128, 16], F32, "abc")
            dDhi = fw.sb(sa, [128, 8, 128], BF16, "dDhi")
            dDlo = fw.sb(sa, [128, 8, 128], BF16, "dDlo")
            U = fw.sb(sa, [128, 128], F32, "U")
            NEGM = fw.sb(sa, [128, 4, 128], BF16, "NEGM")
            fw.load("sp", dtb, dtb[:, :], dtb_d[:, :])
            fw.load("sp", abc, abc[:, :], alog_d[:, :])
            if True:
                stmp = sa
                c4w = fw.sb(stmp, [128, 12, 4], F32, "c4w")
                c4b = fw.sb(stmp, [128, 12], F32, "c4b")
                w4s = fw.sb(stmp, [128, 12, 5], F32, "w4s")
                dfm = fw.sb(stmp, [128, 8], F32, "dfm")
                dhi_b = fw.sb(stmp, [128, 8], BF16, "dhi_b")
                dhi_f = fw.sb(stmp, [128, 8], F32, "dhi_f")
                dlo_f = fw.sb(stmp, [128, 8], F32, "dlo_f")
                negf = fw.sb(stmp, [128, 128], F32, "negf")
                fw.load("sp", c4w, c4w[:, :, :], c4w_d[:, :, :])
                fw.load("sp", c4b, c4b[:, :], c4b_d[:, :])
                fw.load("sp", dfm, dfm[:, :], dfm_d[:, :])
                fw.op("dve", lambda e: e.tensor_scalar_mul(out=w4s[:, :, 0:4], in0=c4w[:, :, :], scalar1=0.5),
                      reads=[c4w], writes=[w4s])
                fw.op("dve", lambda e: e.tensor_scalar_mul(out=w4s[:, :, 4:5], in0=c4b[:, :].unsqueeze(2), scalar1=0.5),
                      reads=[c4b, w4s], writes=[w4s])
                fw.op("dve", lambda e: e.tensor_tensor(
                    out=diag4[:, :, :, :].rearrange("p a b c -> p (a b) c"),
                    in0=identf[:, :].unsqueeze(1).to_broadcast([128, 60, 128]),
                    in1=w4s[:, :, :].rearrange("p a b -> p (a b)").unsqueeze(2).to_broadcast([128, 60, 128]),
                    op=ALU.mult), reads=[identf, w4s], writes=[diag4])
                fw.op("act", lambda e: e.activation(out=abc[:, :], in_=abc[:, :], func=AF.Exp), reads=[abc], writes=[abc])
                fw.op("dve", lambda e: e.tensor_scalar_mul(out=abc[:, :], in0=abc[:, :], scalar1=-1.0), reads=[abc], writes=[abc])
                fw.op("dve", lambda e: e.tensor_copy(out=dhi_b[:, :], in_=dfm[:, :]), reads=[dfm], writes=[dhi_b])
                fw.op("dve", lambda e: e.tensor_copy(out=dhi_f[:, :], in_=dhi_b[:, :]), reads=[dhi_b], writes=[dhi_f])
                fw.op("dve", lambda e: e.tensor_tensor(out=dlo_f[:, :], in0=dfm[:, :], in1=dhi_f[:, :], op=ALU.subtract),
                      reads=[dfm, dhi_f], writes=[dlo_f])
                fw.op("pool", lambda e: e.tensor_tensor(
                    out=dDhi[:, :, :], in0=identf[:, :].unsqueeze(1).to_broadcast([128, 8, 128]),
                    in1=dhi_f[:, :].unsqueeze(2).to_broadcast([128, 8, 128]), op=ALU.mult),
                    reads=[identf, dhi_f], writes=[dDhi])
                fw.op("pool", lambda e: e.tensor_tensor(
                    out=dDlo[:, :, :], in0=identf[:, :].unsqueeze(1).to_broadcast([128, 8, 128]),
                    in1=dlo_f[:, :].unsqueeze(2).to_broadcast([128, 8, 128]), op=ALU.mult),
                    reads=[identf, dlo_f], writes=[dDlo])
                fw.op("pool", lambda e: e.memset(U[:, :], 1.0), writes=[U])
                fw.op("pool", lambda e: e.affine_select(out=U[:, :], in_=U[:, :], pattern=[[1, 128]], compare_op=ALU.is_ge,
                                                        fill=0.0, base=0, channel_multiplier=-1), reads=[U], writes=[U])
                fw.op("pool", lambda e: e.memset(negf[:, :], 0.0), writes=[negf])
                fw.op("pool", lambda e: e.affine_select(out=negf[:, :], in_=negf[:, :], pattern=[[1, 128]],
                                                        compare_op=ALU.is_ge, fill=-30000.0, base=0, channel_multiplier=-1),
                      reads=[negf], writes=[negf])
                fw.op("pool", lambda e: e.tensor_copy(out=NEGM[:, :, :],
                                                      in_=negf[:, :].unsqueeze(1).to_broadcast([128, 4, 128])),
                      reads=[negf], writes=[NEGM])

            lnsA = LNS(sa, "lnA")
            xins = [fw.sb(sa, [128, 1024], F32, "xin") for _ in range(2)]
            _xnb = fw.sb(sa, [128, 1024], BF16, "xnb")
            xnbs = [_xnb, _xnb]
            hTs = [mk_hT(sa, "hT") for _ in range(2)]
            _stg = [fw.sb(sa, [128, 512], BF16, "stage") for _ in range(3)]
            stages = [_stg, _stg]
            xbcs = [fw.sb(sa, [128, 12, 131], BF16, "xbc_fm") for _ in range(2)]
            th = [fw.sb(sa, [128, 512], F32, "th") for _ in range(2)]
            lq = [fw.sb(sa, [128, 512], F32, "lq") for _ in range(2)]
            x_cs = [fw.sb(sa, [128, 12, 128], BF16, "x_c") for _ in range(2)]
            gzs = [fw.sb(sa, [128, 1024], BF16, "gz") for _ in range(2)]
            xdts = [fw.sb(sa, [128, 1024], BF16, "xdt") for _ in range(2)]
            xdds = [fw.sb(sa, [128, 1024], BF16, "xdd") for _ in range(2)]
            B_tms = [fw.sb(sa, [128, 256], BF16, "B_tm") for _ in range(2)]

            def small2(name):
                return [fw.sb(sa, [128, 16], F32, name) for _ in range(2)]
            e1s = small2("e1"); z1s = small2("z1"); w1s = small2("w1"); p1s = small2("p1")
            vts = small2("vt"); a1s = small2("a1"); dtts = small2("dtt"); adts = small2("adt")
            acss = small2("acs"); nacss = small2("nacs"); dds = small2("dd"); dstates = small2("dstate")
            cdts = small2("cdt"); expacss = small2("expacs")
            cbs_t = fw.sb(sa, [128, 256], F32, "cbs")
            adt_his = [fw.sb(sa, [128, 16], BF16, "adt_hi") for _ in range(2)]
            adt_los = [fw.sb(sa, [128, 16], BF16, "adt_lo") for _ in range(2)]
            adt_hfs = [fw.sb(sa, [128, 16], F32, "adt_hf") for _ in range(2)]
            Ub = fw.sb(sa, [128, 128], BF16, "Ub")
            fw.op("pool", lambda e: e.tensor_copy(out=Ub[:, :], in_=U[:, :]), reads=[U], writes=[Ub])
            MT_l = [fw.sb(sa, [128, 16, 128], BF16, "MT") for _ in range(2)]
            S = fw.sb(sa, [128, 1024], F32, "S")
            Sbf = fw.sb(sa, [128, 1024], BF16, "Sbf")
            yt_l = [fw.sb(sa, [128, 1024], F32, "yt") for _ in range(2)]
            ss_l = [fw.sb(sa, [128, 2], F32, "ss") for _ in range(2)]
            ve2_l = [fw.sb(sa, [128, 2], F32, "ve2") for _ in range(2)]
            ti2_l = [fw.sb(sa, [128, 2], I32, "ti2") for _ in range(2)]
            y2_l = [fw.sb(sa, [128, 2], F32, "y2") for _ in range(2)]
            tb2_l = [fw.sb(sa, [128, 2], F32, "tb2") for _ in range(2)]
            _yn = fw.sb(sa, [128, 1024], BF16, "yn")
            yn_l = [_yn, _yn]

            fw.op("pool", lambda e: e.memset(S[:, :], 0.0), writes=[S])
            fw.op("pool", lambda e: e.memset(Sbf[:, :], 0.0), writes=[Sbf])
            for t_ in xbcs:
                fw.op("pool", lambda e, t_=t_: e.memset(t_[:, :, :], 0.0), writes=[t_])

            def A_s0(c, i):
                front_end((xins[i % 2], xnbs[i % 2], hTs[i % 2]), lnsA, c * 128)

            def A_s1(c, i):
                main = c >= NCH
                hT = hTs[i % 2]
                stage = stages[i % 2]
                xbc = xbcs[i % 2]
                xbc_nx = xbcs[(i + 1) % 2]
                vt = vts[i % 2]
                gz = gzs[i % 2]
                for bi, col0 in enumerate((0, 512, 2048)):
                    bk = next_acc()
                    proj(hT, Wssd, col0, 512, bk)
                    if bi % 2 == 0:
                        fw.op("act", lambda e, bk=bk, bi=bi: e.activation(out=stage[bi][:, :],
                                                                          in_=PS[bk][:, :], func=AF.Copy),
                              reads=[PS[bk]], writes=[stage[bi]])
                    else:
                        fw.op("dve", lambda e, bk=bk, bi=bi: e.tensor_copy(out=stage[bi][:, :],
                                                                           in_=PS[bk][:, :]),
                              reads=[PS[bk]], writes=[stage[bi]])
                bk = next_acc()
                proj(hT, Wssd, 2560, 16, bk)
                fw.op("dve", lambda e, bk=bk: e.tensor_tensor(out=vt[:, :], in0=PS[bk][:, 0:16], in1=dtb[:, :], op=ALU.add),
                      reads=[PS[bk], dtb], writes=[vt])
                if main:
                    for zi in range(2):
                        bk = next_acc()
                        proj(hT, Wssd, 1024 + zi * 512, 512, bk)
                        tht = th[zi]
                        fw.op("act", lambda e, bk=bk, tht=tht: e.activation(out=tht[:, :], in_=PS[bk][:, :], func=AF.Tanh,
                                                                            scale=0.5), reads=[PS[bk]], writes=[tht])
                        fw.op("dve", lambda e, bk=bk, tht=tht, zi=zi: e.scalar_tensor_tensor(
                            out=gz[:, zi * 512:(zi + 1) * 512], in0=tht[:, :], scalar=1.0, in1=PS[bk][:, :],
                            op0=ALU.add, op1=ALU.mult), reads=[tht, PS[bk]], writes=[gz])
                for grp, (ct0, nct) in enumerate(((0, 8), (8, 4))):
                    pt = next_pt()
                    for j in range(nct):
                        ct = ct0 + j
                        fw.op("pe", lambda e, j=j, ct=ct, pt=pt: e.transpose(
                            out=psb(pt)[:, j * 128:(j + 1) * 128], in_=stage[ct // 4][:, (ct % 4) * 128:(ct % 4 + 1) * 128],
                            identity=identb[:, :]), reads=[stage[ct // 4], identb], writes=[PS[pt]], signal=(j == nct - 1))
                    eng = "act" if grp == 0 else "dve"
                    if eng == "act":
                        fw.op("act", lambda e, pt=pt, ct0=ct0, nct=nct: e.activation(
                            out=xbc[:, ct0:ct0 + nct, 3:131],
                            in_=psb(pt)[:, 0:nct * 128].rearrange("p (a b) -> p a b", a=nct), func=AF.Copy),
                            reads=[PS[pt]], writes=[xbc])
                    else:
                        fw.op("dve", lambda e, pt=pt, ct0=ct0, nct=nct: e.tensor_copy(
                            out=xbc[:, ct0:ct0 + nct, 3:131],
                            in_=psb(pt)[:, 0:nct * 128].rearrange("p (a b) -> p a b", a=nct)),
                            reads=[PS[pt]], writes=[xbc])
                fw.op("pool", lambda e: e.tensor_copy(out=xbc_nx[:, :, 0:3], in_=xbc[:, :, 128:131]),
                      reads=[xbc, xbc_nx], writes=[xbc_nx])
                if c == NCH - 1:
                    fw.op("pool", lambda e: e.tensor_scalar_mul(out=xbc_nx[:, :, 0:3], in0=xbc_nx[:, :, 0:3],
                                                                scalar1=flag[:, 0:1]), reads=[xbc_nx, flag], writes=[xbc_nx])

            def A_s2(c, i):
                main = c >= NCH
                xbc = xbcs[i % 2]
                x_c = x_cs[i % 2]
                vt = vts[i % 2]; a1 = a1s[i % 2]; dtt = dtts[i % 2]; adt = adts[i % 2]
                acs = acss[i % 2]; nacs = nacss[i % 2]; dd = dds[i % 2]; dstate = dstates[i % 2]
                cdt = cdts[i % 2]; expacs = expacss[i % 2]
                xdt = xdts[i % 2]; xdd = xdds[i % 2]; B_tm = B_tms[i % 2]
                e1 = e1s[i % 2]; z1 = z1s[i % 2]; w1 = w1s[i % 2]; p1 = p1s[i % 2]
                fw.op("act", lambda e: e.activation(out=a1[:, :], in_=vt[:, :], func=AF.Abs), reads=[vt], writes=[a1])
                fw.op("act", lambda e: e.activation(out=e1[:, :], in_=a1[:, :], func=AF.Exp, scale=-1.0), reads=[a1], writes=[e1])
                fw.op("pool", lambda e: e.tensor_scalar_add(out=z1[:, :], in0=e1[:, :], scalar1=2.0), reads=[e1], writes=[z1])
                fw.op("dve", lambda e: e.reciprocal(out=z1[:, :], in_=z1[:, :]), reads=[z1], writes=[z1])
                fw.op("pool", lambda e: e.tensor_tensor(out=z1[:, :], in0=e1[:, :], in1=z1[:, :], op=ALU.mult), reads=[e1, z1], writes=[z1])
                fw.op("pool", lambda e: e.tensor_tensor(out=w1[:, :], in0=z1[:, :], in1=z1[:, :], op=ALU.mult), reads=[z1], writes=[w1])
                fw.op("pool", lambda e: e.tensor_scalar(out=p1[:, :], in0=w1[:, :], scalar1=1.0 / 11.0, scalar2=1.0 / 9.0,
                                                        op0=ALU.mult, op1=ALU.add), reads=[w1], writes=[p1])
                for cf in (1.0 / 7.0, 1.0 / 5.0, 1.0 / 3.0, 1.0):
                    fw.op("pool", lambda e: e.tensor_tensor(out=p1[:, :], in0=p1[:, :], in1=w1[:, :], op=ALU.mult),
                          reads=[p1, w1], writes=[p1])
                    fw.op("pool", lambda e, cf=cf: e.tensor_scalar_add(out=p1[:, :], in0=p1[:, :], scalar1=cf),
                          reads=[p1], writes=[p1])
                fw.op("pool", lambda e: e.tensor_tensor(out=p1[:, :], in0=p1[:, :], in1=z1[:, :], op=ALU.mult),
                      reads=[p1, z1], writes=[p1])
                fw.op("dve", lambda e: e.tensor_scalar_max(out=dtt[:, :], in0=vt[:, :], scalar1=0.0), reads=[vt], writes=[dtt])
                fw.op("dve", lambda e: e.scalar_tensor_tensor(out=dtt[:, :], in0=p1[:, :], scalar=2.0, in1=dtt[:, :],
                                                               op0=ALU.mult, op1=ALU.add), reads=[p1, dtt], writes=[dtt])
                fw.op("pool", lambda e: e.tensor_tensor(out=adt[:, :], in0=dtt[:, :], in1=abc[:, :], op=ALU.mult),
                      reads=[dtt, abc], writes=[adt])
                if main and BC_HILO:
                    adt_hi = adt_his[i % 2]; adt_lo = adt_los[i % 2]; adt_hf = adt_hfs[i % 2]
                    fw.op("pool", lambda e: e.tensor_copy(out=adt_hi[:, :], in_=adt[:, :]), reads=[adt], writes=[adt_hi])
                    fw.op("pool", lambda e: e.tensor_copy(out=adt_hf[:, :], in_=adt_hi[:, :]), reads=[adt_hi], writes=[adt_hf])
                    fw.op("pool", lambda e: e.tensor_tensor(out=adt_lo[:, :], in0=adt[:, :], in1=adt_hf[:, :], op=ALU.subtract),
                          reads=[adt, adt_hf], writes=[adt_lo])
                nct_conv = 12 if main else 10
                for grp in range(3):
                    cts = [ct for ct in range(grp * 4, grp * 4 + 4) if ct < nct_conv]
                    bk = next_acc()
                    for j, ct in enumerate(cts):
                        for k in range(4):
                            fw.op("pe", lambda e, j=j, ct=ct, k=k, bk=bk: e.matmul(
                                PS[bk][:, j * 128:(j + 1) * 128], lhsT=diag4[:, ct, k, :], rhs=xbc[:, ct, k:k + 128],
                                start=(k == 0), stop=False), reads=[diag4, xbc], writes=[PS[bk]], signal=False)
                        fw.op("pe", lambda e, j=j, ct=ct, bk=bk: e.matmul(
                            PS[bk][:, j * 128:(j + 1) * 128], lhsT=diag4[:, ct, 4, :], rhs=onesb[:, :],
                            start=False, stop=True), reads=[diag4, onesb], writes=[PS[bk]], signal=(j == len(cts) - 1))
                    w = len(cts) * 128
                    tht = th[grp % 2]
                    fw.op("act", lambda e, bk=bk, tht=tht, w=w: e.activation(out=tht[:, 0:w], in_=PS[bk][:, 0:w], func=AF.Tanh),
                          reads=[PS[bk]], writes=[tht])
                    fw.op("dve", lambda e, bk=bk, tht=tht, w=w, cts=cts: e.scalar_tensor_tensor(
                        out=x_c[:, cts[0]:cts[0] + len(cts), :].rearrange("p a b -> p (a b)"),
                        in0=tht[:, 0:w], scalar=1.0, in1=PS[bk][:, 0:w], op0=ALU.add, op1=ALU.mult),
                        reads=[tht, PS[bk]], writes=[x_c])
                pq = 4
                fw.op("pe", lambda e: e.matmul(PS[pq][:, 0:16], lhsT=U[:, :], rhs=adt[:, :], start=True, stop=True),
                      reads=[U, adt], writes=[PS[pq]], signal=False)
                fw.op("pe", lambda e: e.matmul(PS[pq][:, 16:32], lhsT=onesf[:, :], rhs=adt[:, :], start=True, stop=True),
                      reads=[onesf, adt], writes=[PS[pq]])
                fw.op("dve", lambda e: e.tensor_copy(out=acs[:, :], in_=PS[pq][:, 0:16]), reads=[PS[pq]], writes=[acs])
                fw.op("dve", lambda e: e.tensor_tensor(out=dd[:, :], in0=PS[pq][:, 16:32], in1=acs[:, :], op=ALU.subtract),
                      reads=[PS[pq], acs], writes=[dd])
                fw.op("act", lambda e: e.activation(out=dstate[:, :], in_=dd[:, :], func=AF.Exp), reads=[dd], writes=[dstate])
                fw.op("act", lambda e: e.activation(out=cdt[:, :], in_=PS[pq][:, 16:32], func=AF.Exp), reads=[PS[pq]], writes=[cdt])
                if main:
                    fw.op("pool", lambda e: e.tensor_scalar_mul(out=nacs[:, :], in0=acs[:, :], scalar1=-1.0),
                          reads=[acs], writes=[nacs])
                    fw.op("act", lambda e: e.activation(out=expacs[:, :], in_=acs[:, :], func=AF.Exp), reads=[acs], writes=[expacs])
                ptx = next_pt()
                for ct in range(8):
                    fw.op("pe", lambda e, ct=ct: e.transpose(out=psb(ptx)[:, ct * 128:(ct + 1) * 128], in_=x_c[:, ct, :],
                                                             identity=identb[:, :]),
                          reads=[x_c, identb], writes=[PS[ptx]], signal=(ct == 7))
                fw.op("dve", lambda e: e.tensor_tensor(
                    out=xdt[:, :].rearrange("p (h q) -> p h q", h=16),
                    in0=psb(ptx)[:, :].rearrange("p (h q) -> p h q", h=16),
                    in1=dtt[:, :].unsqueeze(2).to_broadcast([128, 16, 64]), op=ALU.mult),
                    reads=[PS[ptx], dtt], writes=[xdt])
                fw.op("pool", lambda e: e.tensor_tensor(
                    out=xdd[:, :].rearrange("p (h q) -> p h q", h=16),
                    in0=xdt[:, :].rearrange("p (h q) -> p h q", h=16),
                    in1=dstate[:, :].unsqueeze(2).to_broadcast([128, 16, 64]), op=ALU.mult),
                    reads=[xdt, dstate], writes=[xdd])
                ptb = next_pt()
                for j in range(2):
                    fw.op("pe", lambda e, j=j: e.transpose(out=psb(ptb)[:, j * 128:(j + 1) * 128], in_=x_c[:, 8 + j, :],
                                                           identity=identb[:, :]),
                          reads=[x_c, identb], writes=[PS[ptb]], signal=(j == 1))
                fw.op("act", lambda e: e.activation(out=B_tm[:, :], in_=psb(ptb)[:, 0:256], func=AF.Copy),
                      reads=[PS[ptb]], writes=[B_tm])

            def A_s3(c, i):
                main = c >= NCH
                m = c - NCH
                x_c = x_cs[i % 2]
                adt = adts[i % 2]; nacs = nacss[i % 2]; cdt = cdts[i % 2]; expacs = expacss[i % 2]
                xdt = xdts[i % 2]; xdd = xdds[i % 2]; B_tm = B_tms[i % 2]
                gz = gzs[i % 2]
                MT = MT_l[i % 2]; yt = yt_l[i % 2]; yn = yn_l[i % 2]; ss = ss_l[i % 2]; ve2 = ve2_l[i % 2]
                ti2 = ti2_l[i % 2]; y2 = y2_l[i % 2]; tb2 = tb2_l[i % 2]
                if main:
                    pcb = 5
                    for g in range(2):
                        fw.op("pe", lambda e, g=g: e.matmul(PS[pcb][:, g * 128:(g + 1) * 128], lhsT=x_c[:, 8 + g, :],
                                                            rhs=x_c[:, 10 + g, :], start=True, stop=True),
                              reads=[x_c], writes=[PS[pcb]], signal=(g == 1))
                    fw.op("act", lambda e: e.activation(out=cbs_t[:, 0:256], in_=PS[pcb][:, 0:256], func=AF.Copy),
                          reads=[PS[pcb]], writes=[cbs_t])
                    for q in range(4):
                        bq = 4 + (q % 2)
                        g = q // 2
                        fw.op("pe", lambda e, bq=bq: e.matmul(
                            PS[bq][:, :], lhsT=identb[:, :], rhs=NEGM[:, :, :].rearrange("p a b -> p (a b)"),
                            start=True, stop=False), reads=[identb, NEGM], writes=[PS[bq]], signal=False)
                        if BC_HILO:
                            adt_hi = adt_his[i % 2]; adt_lo = adt_los[i % 2]
                            for j in range(4):
                                h = 4 * q + j
                                fw.op("pe", lambda e, j=j, h=h, bq=bq: e.matmul(
                                    PS[bq][:, j * 128:(j + 1) * 128], lhsT=adt_hi[:, h:h + 1].to_broadcast([128, 128]),
                                    rhs=Ub[:, :], start=False, stop=False), reads=[adt_hi, Ub], writes=[PS[bq]], signal=False)
                                fw.op("pe", lambda e, j=j, h=h, bq=bq: e.matmul(
                                    PS[bq][:, j * 128:(j + 1) * 128], lhsT=adt_lo[:, h:h + 1].to_broadcast([128, 128]),
                                    rhs=Ub[:, :], start=False, stop=(j == 3)), reads=[adt_lo, Ub], writes=[PS[bq]],
                                    signal=(j == 3))
                        else:
                            for j in range(4):
                                h = 4 * q + j
                                fw.op("pe", lambda e, j=j, h=h, bq=bq: e.matmul(
                                    PS[bq][:, j * 128:(j + 1) * 128], lhsT=adt[:, h:h + 1].to_broadcast([128, 128]),
                                    rhs=U[:, :], start=False, stop=(j == 3)), reads=[adt, U], writes=[PS[bq]],
                                    signal=(j == 3))
                        lqt = lq[q % 2]
                        for j in range(4):
                            h = 4 * q + j
                            fw.op("act", lambda e, j=j, h=h, bq=bq, lqt=lqt: e.activation(
                                out=lqt[:, j * 128:(j + 1) * 128], in_=PS[bq][:, j * 128:(j + 1) * 128], func=AF.Exp,
                                bias=nacs[:, h:h + 1]), reads=[PS[bq], nacs], writes=[lqt])
                        fw.op(MT_ENG, lambda e, q=q, g=g, lqt=lqt: e.tensor_tensor(
                            out=MT[:, 4 * q:4 * q + 4, :], in0=lqt[:, :].rearrange("p (a b) -> p a b", a=4),
                            in1=cbs_t[:, g * 128:(g + 1) * 128].unsqueeze(1).to_broadcast([128, 4, 128]), op=ALU.mult),
                            reads=[lqt, cbs_t], writes=[MT])
                    for h in range(16):
                        g = h // 8
                        yb = 6 + g
                        o = (h % 8) * 64
                        ct = h // 2
                        co = (h % 2) * 64
                        fw.op("pe", lambda e, yb=yb, o=o, ct=ct, co=co: e.matmul(
                            PS[yb][:, o:o + 64], lhsT=x_c[:, ct, :], rhs=dDhi[:, ct, co:co + 64], start=True, stop=False),
                            reads=[x_c, dDhi], writes=[PS[yb]], signal=False)
                        fw.op("pe", lambda e, yb=yb, o=o, ct=ct, co=co: e.matmul(
                            PS[yb][:, o:o + 64], lhsT=x_c[:, ct, :], rhs=dDlo[:, ct, co:co + 64], start=False, stop=False),
                            reads=[x_c, dDlo], writes=[PS[yb]], signal=False)
                        fw.op("pe", lambda e, yb=yb, o=o, h=h: e.matmul(
                            PS[yb][:, o:o + 64], lhsT=MT[:, h, :], rhs=xdt[:, h * 64:(h + 1) * 64], start=False, stop=True),
                            reads=[MT, xdt], writes=[PS[yb]], signal=(h % 8 == 7))
                    for g in range(2):
                        fw.op("pe", lambda e, g=g: e.matmul(PS[g][:, :], lhsT=x_c[:, 10 + g, :],
                                                            rhs=Sbf[:, g * 512:(g + 1) * 512], start=True, stop=True),
                              reads=[x_c, Sbf], writes=[PS[g]])
                        fw.op("dve", lambda e, g=g: e.tensor_tensor(
                            out=yt[:, g * 512:(g + 1) * 512].rearrange("p (h q) -> p h q", h=8),
                            in0=PS[g][:, :].rearrange("p (h q) -> p h q", h=8),
                            in1=expacs[:, g * 8:(g + 1) * 8].unsqueeze(2).to_broadcast([128, 8, 64]), op=ALU.mult),
                            reads=[PS[g], expacs], writes=[yt])
                        fw.op("dve", lambda e, g=g: e.tensor_tensor(
                            out=yt[:, g * 512:(g + 1) * 512], in0=yt[:, g * 512:(g + 1) * 512], in1=PS[6 + g][:, :],
                            op=ALU.add), reads=[yt, PS[6 + g]], writes=[yt])
                    fw.op("pool", lambda e: e.tensor_tensor(out=yt[:, :], in0=yt[:, :], in1=gz[:, :], op=ALU.mult),
                          reads=[yt, gz], writes=[yt])
                    fw.op("pool", lambda e: e.memset(ss[:, :], 0.0), writes=[ss])
                    for g in range(2):
                        junk = lq[g]
                        fw.op("act", lambda e, g=g, junk=junk: e.activation(out=junk[:, :], in_=yt[:, g * 512:(g + 1) * 512],
                                                                            func=AF.Square, accum_out=ss[:, g:g + 1]),
                              reads=[yt, ss], writes=[junk, ss])
                    fw.op("dve", lambda e: e.tensor_scalar(out=ve2[:, :], in0=ss[:, :], scalar1=1.0 / 512.0,
                                                           scalar2=4.0 * RMS_EPS, op0=ALU.mult, op1=ALU.add),
                          reads=[ss], writes=[ve2])
                    r2 = rsqrt_chain((ti2, y2, tb2), ve2, 2)
                    for g in range(2):
                        fw.op("act", lambda e, g=g: e.activation(out=yn[:, g * 512:(g + 1) * 512],
                                                                 in_=yt[:, g * 512:(g + 1) * 512], func=AF.Identity,
                                                                 scale=r2[:, g:g + 1]), reads=[yt, r2], writes=[yn])
                    pty = next_pt()
                    for k in range(8):
                        fw.op("pe", lambda e, k=k: e.transpose(out=psb(pty)[:, k * 128:(k + 1) * 128],
                                                               in_=yn[:, k * 128:(k + 1) * 128], identity=identb[:, :]),
                              reads=[yn, identb], writes=[PS[pty]], signal=(k == 7))
                    fw.op("act", lambda e, m=m: e.activation(
                        out=YS[:, :, m * 128:(m + 1) * 128],
                        in_=psb(pty)[:, :].rearrange("p (a b) -> p a b", a=8), func=AF.Copy),
                        reads=[PS[pty]], writes=[YS])
                if c < 2 * NCH - 1:
                    fw.op("pool", lambda e: e.tensor_tensor(
                        out=S[:, :].rearrange("p (h q) -> p h q", h=16),
                        in0=S[:, :].rearrange("p (h q) -> p h q", h=16),
                        in1=cdt[:, :].unsqueeze(2).to_broadcast([128, 16, 64]), op=ALU.mult),
                        reads=[S, cdt], writes=[S])
                    for g in range(2):
                        bs = 2 + g
                        fw.op("pe", lambda e, g=g, bs=bs: e.matmul(PS[bs][:, :], lhsT=B_tm[:, g * 128:(g + 1) * 128],
                                                                   rhs=xdd[:, g * 512:(g + 1) * 512], start=True, stop=True),
                              reads=[B_tm, xdd], writes=[PS[bs]])
                        fw.op("dve", lambda e, g=g, bs=bs: e.tensor_tensor(
                            out=S[:, g * 512:(g + 1) * 512], in0=S[:, g * 512:(g + 1) * 512], in1=PS[bs][:, :], op=ALU.add),
                            reads=[S, PS[bs]], writes=[S])
                    if c == NCH - 1:
                        fw.op("pool", lambda e: e.tensor_scalar_mul(out=S[:, :], in0=S[:, :], scalar1=flag[:, 0:1]),
                              reads=[S, flag], writes=[S])
                    fw.op("act", lambda e: e.activation(out=Sbf[:, :], in_=S[:, :], func=AF.Copy), reads=[S], writes=[Sbf])

            evac_eng[0] = "dve"
            NEWTON_CUR[0] = NEWTON_A
            pipeline([A_s0, A_s1, A_s2, A_s3], list(range(2 * NCH)))
            if PASS_BARRIER:
                fw.barrier()

        if debug:
            with ExitStack() as sd:
                dtile = fw.sb(sd, [128, 8 * NM], F32, "dtile")
                fw.op("dve", lambda e: e.tensor_copy(out=dtile[:, :], in_=YS[:, :, :].rearrange("p a b -> p (a b)")),
                      reads=[YS], writes=[dtile])
                fw.store("sp", dtile, dbg_ys[:, :], dtile[:, :])
                dtile2 = fw.sb(sd, [128, 8 * NM], F32, "dtile2")
                fw.op("dve", lambda e: e.tensor_copy(out=dtile2[:, :], in_=YC[:, :, :].rearrange("p a b -> p (a b)")),
                      reads=[YC], writes=[dtile2])
                fw.store("sp", dtile2, dbg_yc[:, :], dtile2[:, :])
                fw.barrier()

        with ExitStack() as sc:
            ngfm = fw.sb(sc, [128, 8], F32, "ngfm")
            g1fm = fw.sb(sc, [128, 8], F32, "g1fm")
            b1fm = fw.sb(sc, [128, 8], F32, "b1fm")
            G0 = fw.sb(sc, [128, 1024], F32, "G0")
            G1 = fw.sb(sc, [128, 1024], F32, "G1")
            B1 = fw.sb(sc, [128, 1024], F32, "B1")
            G2 = fw.sb(sc, [128, 1024], F32, "G2")
            B2 = fw.sb(sc, [128, 1024], F32, "B2")
            B0rows = fw.sb(sc, [33, 1024], BF16, "B0rows")
            fw.load("sp", ngfm, ngfm[:, :], ngfm_d[:, :])
            fw.load("sp", g1fm, g1fm[:, :], g1fm_d[:, :])
            fw.load("sp", b1fm, b1fm[:, :], b1fm_d[:, :])
            fw.op("pool", lambda e: e.memset(B0rows[:, :], 0.0), writes=[B0rows])
            Wo = WBlocks(sc, "Wo", w_out, 16, [(0, 512), (512, 512)])
            Wg = WBlocks(sc, "Wg", w_gate, 8, [(0, 512), (512, 512)])
            Wp = WBlocks(sc, "Wp", w_ple, 2, [(0, 1024)])
            for nb in range(2):
                wt_ = Wo.tiles[nb]
                for k in range(8):
                    if (k + nb) % 2 == 0:
                        fw.op("act", lambda e, k=k, wt_=wt_: e.activation(out=wt_[:, k, :], in_=wt_[:, k, :], func=AF.Copy,
                                                                          scale=ngfm[:, k:k + 1]), reads=[wt_, ngfm], writes=[wt_])
                    else:
                        fw.op("dve", lambda e, k=k, wt_=wt_: e.tensor_scalar_mul(out=wt_[:, k, :], in0=wt_[:, k, :],
                                                                                 scalar1=ngfm[:, k:k + 1]),
                              reads=[wt_, ngfm], writes=[wt_])
            hilo_rows(sc, B0rows, [(ALPHA, b0row_d), (1.0, boutrow_d)], 1024, CH=512)
            fw.load("sp", G0, G0[:, :], g0bc_d[:, :])
            fw.load("sp", G1, G1[:, :], g1bc_d[:, :])
            fw.load("sp", B1, B1[:, :], b1bc_d[:, :])
            fw.load("sp", G2, G2[:, :], g2bc_d[:, :])
            fw.load("sp", B2, B2[:, :], b2bc_d[:, :])
            fw.op("pool", lambda e: e.tensor_scalar_mul(out=G0[:, :], in0=G0[:, :], scalar1=ALPHA), reads=[G0], writes=[G0])
            fw.op("pool", lambda e: e.tensor_scalar_mul(out=G1[:, :], in0=G1[:, :], scalar1=ALPHA), reads=[G1], writes=[G1])
            fw.op("pool", lambda e: e.tensor_scalar_mul(out=B1[:, :], in0=B1[:, :], scalar1=ALPHA), reads=[B1], writes=[B1])

            lnsC = LNS(sc, "lnC", nsets=6)
            lnsC.engs = LNC_ENGS
            xins = [fw.sb(sc, [128, 1024], F32, "xinC") for _ in range(2)]
            pins = [fw.sb(sc, [128, 256], F32, "pin") for _ in range(2)]
            r0s = [fw.sb(sc, [128, 1024], F32, "r0") for _ in range(2)]
            n1s = [fw.sb(sc, [128, 1024], F32, "n1") for _ in range(2)]
            n1b = fw.sb(sc, [128, 1024], BF16, "n1b")
            h1Ts = [mk_hT(sc, "h1T") for _ in range(2)]
            pb = fw.sb(sc, [128, 256], BF16, "pb")
            pTs = [fw.sb(sc, [128, 2, 128], BF16, "pT") for _ in range(2)]
            tgate = [fw.sb(sc, [128, 512], F32, "tgate") for _ in range(2)]
            r2t = fw.sb(sc, [128, 1024], F32, "r2t")
            outs = [fw.sb(sc, [128, 1024], F32, "outt") for _ in range(2)]
            out_toks = []

            def C_s0(m, i):
                xin = xins[i % 2]
                pin = pins[i % 2]
                r0 = r0s[i % 2]
                pT = pTs[i % 2]
                fw.load("sp", xin, xin[:, :], xa[(NCH + m) * 128:(NCH + m + 1) * 128, :])
                fw.load("sp", pin, pin[:, :], pa[m * 128:(m + 1) * 128, :])
                rstd, nmr = lnsC.run(xin)
                fw.op("act", lambda e: e.activation(out=r0[:, :], in_=xin[:, :], func=AF.Identity, bias=nmr[:, :],
                                                    scale=rstd[:, :]), reads=[xin, nmr, rstd], writes=[r0])
                fw.op(C_TT, lambda e: e.tensor_tensor(out=r0[:, :], in0=r0[:, :], in1=G0[:, :], op=ALU.mult),
                      reads=[r0, G0], writes=[r0])
                if C_PB == "act":
                    fw.op("act", lambda e: e.activation(out=pb[:, :], in_=pin[:, :], func=AF.Copy), reads=[pin], writes=[pb])
                else:
                    fw.op("pool", lambda e: e.tensor_copy(out=pb[:, :], in_=pin[:, :]), reads=[pin], writes=[pb])
                pt2 = next_pt()
                for k in range(2):
                    fw.op("pe", lambda e, k=k: e.transpose(out=psb(pt2)[:, k * 128:(k + 1) * 128],
                                                           in_=pb[:, k * 128:(k + 1) * 128], identity=identb[:, :]),
                          reads=[pb, identb], writes=[PS[pt2]], signal=(k == 1))
                fw.op("dve", lambda e: e.tensor_copy(out=pT[:, :, :],
                                                     in_=psb(pt2)[:, 0:256].rearrange("p (a b) -> p a b", a=2)),
                      reads=[PS[pt2]], writes=[pT])

            def C_s1(m, i):
                r0 = r0s[i % 2]
                n1 = n1s[i % 2]
                h1T = h1Ts[i % 2]
                for nb in range(2):
                    fw.op("pe", lambda e, nb=nb: e.matmul(PS[nb][:, :], lhsT=ones33[:, :],
                                                          rhs=B0rows[:, nb * 512:(nb + 1) * 512], start=True, stop=False),
                          reads=[ones33, B0rows], writes=[PS[nb]], signal=False)
                    for kt in range(16):
                        src = YS if kt < 8 else YC
                        fw.op("pe", lambda e, nb=nb, kt=kt, src=src: e.matmul(
                            PS[nb][:, :], lhsT=src[:, kt % 8, m * 128:(m + 1) * 128],
                            rhs=Wo.get(kt, nb * 512, 512)[1],
                            start=False, stop=(kt == 15)), reads=[src, Wo.get(kt, nb * 512, 512)[0]], writes=[PS[nb]],
                            signal=(kt == 15))
                    fw.op("dve", lambda e, nb=nb: e.tensor_tensor(out=r0[:, nb * 512:(nb + 1) * 512],
                                                                  in0=r0[:, nb * 512:(nb + 1) * 512], in1=PS[nb][:, :],
                                                                  op=ALU.add), reads=[r0, PS[nb]], writes=[r0])
                rstd1, nmr1 = lnsC.run(r0)
                fw.op("act", lambda e: e.activation(out=n1b[:, :], in_=r0[:, :], func=AF.Identity, bias=nmr1[:, :],
                                                    scale=rstd1[:, :]), reads=[r0, nmr1, rstd1], writes=[n1b])
                fw.op("act", lambda e: e.activation(out=n1[:, :], in_=r0[:, :], func=AF.Identity, bias=nmr1[:, :],
                                                    scale=rstd1[:, :]), reads=[r0, nmr1, rstd1], writes=[n1])
                pt = next_pt()
                for k in range(8):
                    fw.op("pe", lambda e, k=k: e.transpose(out=psb(pt)[:, k * 128:(k + 1) * 128],
                                                           in_=n1b[:, k * 128:(k + 1) * 128], identity=identb[:, :]),
                          reads=[n1b, identb], writes=[PS[pt]], signal=(k == 7))
                evac_T(pt, h1T, g1fm, b1fm)
                fw.op(C_TT, lambda e: e.tensor_tensor(out=n1[:, :], in0=n1[:, :], in1=G1[:, :], op=ALU.mult),
                      reads=[n1, G1], writes=[n1])
                fw.op("pool", lambda e: e.tensor_tensor(out=n1[:, :], in0=n1[:, :], in1=B1[:, :], op=ALU.add),
                      reads=[n1, B1], writes=[n1])

            def C_s2(m, i):
                n1 = n1s[i % 2]
                h1T = h1Ts[i % 2]
                pT = pTs[i % 2]
                for nb in range(2):
                    bgk = 4 + nb
                    for kt in range(8):
                        fw.op("pe", lambda e, nb=nb, kt=kt, bgk=bgk: e.matmul(
                            PS[bgk][:, :], lhsT=hsl(h1T, kt), rhs=Wg.get(kt, nb * 512, 512)[1],
                            start=(kt == 0), stop=(kt == 7)), reads=[h1T, Wg.get(kt, nb * 512, 512)[0]], writes=[PS[bgk]],
                            signal=(kt == 7))
                    tgt = tgate[nb]
                    fw.op("act", lambda e, bgk=bgk, tgt=tgt: e.activation(out=tgt[:, :], in_=PS[bgk][:, :], func=AF.Tanh,
                                                                          scale=0.5), reads=[PS[bgk]], writes=[tgt])
                    bpk = 6 + nb
                    for kt in range(2):
                        fw.op("pe", lambda e, nb=nb, kt=kt, bpk=bpk: e.matmul(
                            PS[bpk][:, :], lhsT=pT[:, kt, :], rhs=Wp.get(kt, nb * 512, 512)[1],
                            start=(kt == 0), stop=(kt == 1)), reads=[pT, Wp.tiles[0]], writes=[PS[bpk]], signal=(kt == 1))
                    fw.op("dve", lambda e, nb=nb, bpk=bpk, tgt=tgt: e.scalar_tensor_tensor(
                        out=r2t[:, nb * 512:(nb + 1) * 512], in0=tgt[:, :], scalar=1.0, in1=PS[bpk][:, :],
                        op0=ALU.add, op1=ALU.mult), reads=[tgt, PS[bpk]], writes=[r2t])
                    fw.op("dve", lambda e, nb=nb: e.scalar_tensor_tensor(
                        out=r2t[:, nb * 512:(nb + 1) * 512], in0=r2t[:, nb * 512:(nb + 1) * 512], scalar=0.5,
                        in1=n1[:, nb * 512:(nb + 1) * 512], op0=ALU.mult, op1=ALU.add), reads=[r2t, n1], writes=[r2t])
                rstd2, nmr2 = lnsC.run(r2t)
                ot = outs[i % 2]
                fw.op("act", lambda e, ot=ot: e.activation(out=ot[:, :], in_=r2t[:, :], func=AF.Identity, bias=nmr2[:, :],
                                                           scale=rstd2[:, :]), reads=[r2t, nmr2, rstd2], writes=[ot])
                fw.op("pool", lambda e, ot=ot: e.tensor_tensor(out=ot[:, :], in0=ot[:, :], in1=G2[:, :], op=ALU.mult),
                      reads=[ot, G2], writes=[ot])
                fw.op(C_B2, lambda e, ot=ot: e.tensor_tensor(out=ot[:, :], in0=ot[:, :], in1=B2[:, :], op=ALU.add),
                      reads=[ot, B2], writes=[ot])
                out_toks.append(fw.store("sp", ot, out_d[m * 128:(m + 1) * 128, :], ot[:, :]))

            evac_eng[0] = EVAC_BC
            pipeline([C_s0, C_s1, C_s2], list(range(NCH)))
            fw.wait_all("sp", out_toks)
            fw.barrier()
    return nc


def _fm(v, nt):
    return np.ascontiguousarray(np.asarray(v, dtype=np.float32).reshape(nt, 128).T)


def _bc(v):
    v = np.asarray(v, dtype=np.float32).reshape(1, -1)
    return np.ascontiguousarray(np.broadcast_to(v, (128, v.shape[1])))


def make_core_inputs(inputs, b, hf, NCH=NCH_FULL):
    half = NCH * 128
    x = np.asarray(inputs["x"], dtype=np.float32)
    p = np.asarray(inputs["p"], dtype=np.float32)
    if hf == 0:
        xa = np.concatenate([np.zeros((half, D_MODEL), np.float32), x[b, 0:half]], axis=0)
        pa = p[0, b, 0:half]
        fl = 0.0
    else:
        xa = x[b, 0:2 * half]
        pa = p[0, b, half:2 * half]
        fl = 1.0
    c4w = np.asarray(inputs["ssm_conv_w"], np.float32)[0]
    c31w = np.asarray(inputs["conf_conv_w"], np.float32)[0]
    d = {
        "xa": np.ascontiguousarray(xa), "pa": np.ascontiguousarray(pa),
        "flag": np.full((128, 1), fl, np.float32),
        "w_in": np.ascontiguousarray(np.asarray(inputs["w_in"], np.float32)[0]),
        "w_out": np.ascontiguousarray(np.asarray(inputs["w_out"], np.float32)[0]),
        "w_gate": np.ascontiguousarray(np.asarray(inputs["w_ple_gate"], np.float32)[0]),
        "w_ple": np.ascontiguousarray(np.asarray(inputs["w_ple_proj"], np.float32)[0]),
        "g0fm": _fm(inputs["ln_emb_g"], 8), "b0fm": _fm(inputs["ln_emb_b"], 8),
        "g0bc": _bc(inputs["ln_emb_g"]),
        "b0row": np.ascontiguousarray(np.asarray(inputs["ln_emb_b"], np.float32).reshape(1, 1024)),
        "boutrow": np.ascontiguousarray(np.asarray(inputs["b_out"], np.float32).reshape(1, 1024)),
        "c4w": np.ascontiguousarray(c4w.T.reshape(12, 128, 4).transpose(1, 0, 2)),
        "c4b": _fm(np.asarray(inputs["ssm_conv_b"])[0], 12),
        "dtb": _bc(inputs["dt_bias"]), "alog": _bc(inputs["a_log"]),
        "dfm": _fm(np.repeat(np.asarray(inputs["d_skip"], np.float32).reshape(16), 64), 8),
        "ngfm": _fm(np.asarray(inputs["ssm_norm_g"])[0], 8),
        "bglu": np.ascontiguousarray(np.asarray(inputs["b_glu"], np.float32).reshape(1, 2048)),
        "c31w": np.ascontiguousarray(c31w.T.reshape(8, 128, 31).transpose(1, 0, 2)),
        "c31b": _fm(np.asarray(inputs["conf_conv_b"])[0], 8),
        "clg": _fm(np.asarray(inputs["conf_ln_g"])[0], 8), "clb": _fm(np.asarray(inputs["conf_ln_b"])[0], 8),
        "g1fm": _fm(np.asarray(inputs["ln1_g"])[0], 8), "b1fm": _fm(np.asarray(inputs["ln1_b"])[0], 8),
        "g1bc": _bc(inputs["ln1_g"]), "b1bc": _bc(inputs["ln1_b"]),
        "g2bc": _bc(inputs["ln2_g"]), "b2bc": _bc(inputs["ln2_b"]),
    }
    return d


_NC_CACHE = {}


def kernel(**inputs):
    if "full" not in _NC_CACHE:
        _NC_CACHE["full"] = build(NCH_FULL)
    nc = _NC_CACHE["full"]
    in_maps = []
    for core in range(8):
        b, hf = core // 2, core % 2
        in_maps.append(make_core_inputs(inputs, b, hf))
    res = run_bass_kernel_spmd(nc, in_maps, core_ids=list(range(8)))
    out = np.zeros((BATCH, SEQ, D_MODEL), np.float32)
    for core in range(8):
        b, hf = core // 2, core % 2
        out[b, hf * 2048:(hf + 1) * 2048] = res.results[core]["out"]
    return out
```

```python
import numpy as np
from contextlib import ExitStack
import concourse.bass as bass
import concourse.mybir as mybir
from concourse.bass_utils import run_bass_kernel_spmd

F32 = mybir.dt.float32
BF16 = mybir.dt.bfloat16
I32 = mybir.dt.int32
AF = mybir.ActivationFunctionType
ALU = mybir.AluOpType

D_MODEL = 1024
SEQ = 4096
BATCH = 4
D_PLE = 256
NE = 5648
N_SSD = 2576
N_CONF = 3072
ALPHA = 2.0 ** 0.25
LN_EPS = 1e-5
RMS_EPS = 1e-5
NCH_FULL = 16


class Tl:
    def __init__(self, ap, name):
        self.ap = ap
        self.name = name
        self.last_w = None
        self.readers = []
        self.dma_sem = None
        self.dma_cnt = 0
        self.is_psum = False
        self.alias_preds = set()
        self.addr = None
        self.nbytes = 0

    def __getitem__(self, k):
        return self.ap[k]


import os as _os
NO_SCHED = bool(int(_os.environ.get("KNOSCHED", "0")))
SCHED_PRIO = _os.environ.get("KPRIO", "bl")
SCHED_WIN = float(_os.environ.get("KWIN", "0.5"))
SCHED_TRIES = int(_os.environ.get("KTRIES", "6"))
SOFT_ENGS = tuple(x for x in _os.environ.get("KSOFT", "dve").split(",") if x)
SERIALIZE_PSUM_READS = bool(int(_os.environ.get("KPSUMSER", "1")))
ALIAS_DEPS = bool(int(_os.environ.get("KALIAS", "1")))


class _Cap:
    def __init__(self):
        self.rec = None

    def __getattr__(self, name):
        def f(*args, **kw):
            self.rec = (name, args, kw)
            return self
        return f

    def then_inc(self, *a, **k):
        return self


def _free_elems(ap):
    n = 1
    for d in ap.shape[1:]:
        n *= int(d)
    return n


def _est_cost(eng, fn):
    cap = _Cap()
    try:
        fn(cap)
    except Exception:
        return 0.3
    if cap.rec is None:
        return 0.3
    name, args, kw = cap.rec
    out = kw.get("out", args[0] if args else None)
    try:
        n = _free_elems(out)
    except Exception:
        n = 128
    if eng == "pe":
        if name == "transpose":
            return 0.10
        lhsT = kw.get("lhsT")
        mult = 4.0 if (lhsT is not None and lhsT.dtype == F32) else 1.0
        return mult * max(0.045, n / 2400.0 + 0.01)
    if eng == "act":
        return max(0.45, 0.1 + n * 0.00118)
    if eng == "dve":
        return 0.28 + n / 960.0
    if eng == "pool":
        if name == "tensor_copy":
            return 0.40 + n * 0.0032
        if name == "memset":
            return 0.30 + n * 0.0006
        return 0.40 + n * 0.0025
    return 0.3


class _Op:
    __slots__ = ("eng", "fns", "reads", "writes", "cost", "preds", "kind", "tile", "lat", "idx", "tok",
                 "succs", "npred", "start", "finish", "dma_args", "label", "hard")

    def __init__(self, eng, kind):
        self.eng = eng
        self.kind = kind
        self.fns = []
        self.reads = []
        self.writes = []
        self.cost = 0.0
        self.preds = set()
        self.tile = None
        self.lat = 0.0
        self.tok = None
        self.dma_args = None


class FW:
    ENGS = ("pe", "act", "dve", "pool", "sp")

    def __init__(self, nc, stack):
        self.nc = nc
        self.stack = stack
        self.emap = {"pe": nc.tensor, "act": nc.scalar, "dve": nc.vector, "pool": nc.gpsimd, "sp": nc.sync}
        self.cnt = {e: 0 for e in self.ENGS}
        self.sems = {}
        for e in self.ENGS:
            self.sems[e] = stack.enter_context(nc.semaphore("s_" + e))
        self.waited = {e: {} for e in self.ENGS}
        self.ntile = 0
        self.tiles = []
        self.sb_tiles = []
        self.dma_tiles = []
        self.ops = []
        self.pend = None
        self.sim_time = 0.0
        self.log = None
        self.idle = {}
        self.cur_label = None

    def sb(self, stack, shape, dt, name, side=None):
        self.ntile += 1
        nm = f"{name}_{self.ntile}"
        if side is None:
            t = stack.enter_context(self.nc.sbuf_tensor(nm, list(shape), dt))
        else:
            t = stack.enter_context(self.nc.sbuf_tensor(nm, list(shape), dt, side=side))
        tl = Tl(t, nm)
        try:
            ml = self.nc.lookup_mloc(t)
            tl.addr = int(ml.addr)
            tl.nbytes = int(ml.dims[1])
        except Exception:
            tl.addr = None
            tl.nbytes = 0
        if ALIAS_DEPS and tl.addr is not None:
            inh = set()
            for o in self.sb_tiles:
                if o.addr is None:
                    continue
                if o.addr < tl.addr + tl.nbytes and tl.addr < o.addr + o.nbytes:
                    if o.last_w is not None:
                        inh.add(o.last_w)
                    inh.update(o.readers)
                    inh.update(o.alias_preds)
            tl.alias_preds = inh
        self.sb_tiles.append(tl)
        self.tiles.append(tl)
        return tl

    def ps(self, stack, shape, dt, name):
        self.ntile += 1
        nm = f"{name}_{self.ntile}"
        t = stack.enter_context(self.nc.psum_tensor(nm, list(shape), dt))
        tl = Tl(t, nm)
        tl.is_psum = True
        self.tiles.append(tl)
        return tl

    def _deps(self, op):
        for t in list(op.reads) + list(op.writes):
            if t.alias_preds:
                op.preds.update(t.alias_preds)
        for t in op.writes:
            if t.alias_preds and not (op.kind == "ld"):
                pass
        for t in op.reads:
            if t.last_w is not None:
                op.preds.add(t.last_w)
            if t.is_psum and SERIALIZE_PSUM_READS:
                for r in t.readers:
                    if self.ops[r].eng != op.eng:
                        op.preds.add(r)
        op.hard = set(op.preds)
        for t in op.writes:
            if t.last_w is not None:
                op.preds.add(t.last_w)
            op.preds.update(t.readers)

    def _commit(self, op):
        op.label = self.cur_label
        op.idx = len(self.ops)
        op.preds.discard(op.idx)
        self.ops.append(op)
        for t in op.reads:
            if t not in op.writes:
                t.readers.append(op.idx)
        for t in op.writes:
            t.last_w = op.idx
            t.readers = []
            if t.alias_preds and op.kind != "ld":
                t.alias_preds = set()

    COST_SCALE = {k: float(v) for k, v in (kv.split("=") for kv in _os.environ.get("KCOST", "").split(",") if kv)}

    def op(self, eng, fn, reads=(), writes=(), signal=True):
        if self.pend is not None and self.pend.eng != eng:
            raise RuntimeError("unsignaled group must be closed on the same engine")
        if self.pend is None:
            o = _Op(eng, "c")
        else:
            o = self.pend
        o.fns.append(fn)
        for t in reads:
            if t not in o.reads:
                o.reads.append(t)
        for t in writes:
            if t not in o.writes:
                o.writes.append(t)
        o.cost += _est_cost(eng, fn) * self.COST_SCALE.get(eng, 1.0)
        if not signal:
            self.pend = o
            return None
        self.pend = None
        self._deps(o)
        self._commit(o)
        return o

    def _dsem(self, t):
        if t.dma_sem is None:
            key = "dma_" + t.name
            self.sems[key] = self.stack.enter_context(self.nc.semaphore(key))
            t.dma_sem = key
            self.dma_tiles.append(t)

    def load(self, eng, t, out_ap, in_ap, after=()):
        self._dsem(t)
        o = _Op(eng, "ld")
        o.preds.update(p.idx for p in after if p is not None)
        o.tile = t
        o.writes = [t]
        o.dma_args = (out_ap, in_ap)
        try:
            nbytes = _free_elems(out_ap) * 4
        except Exception:
            nbytes = 4096
        o.cost = 0.08 if eng == "sp" else 0.6
        o.lat = 2.0 + nbytes * 128 / 150e3
        prev = t.last_w
        if False and eng == "sp" and prev is not None and self.ops[prev].kind == "ld" and not t.readers:
            o.preds = set(self.ops[prev].preds)
        else:
            self._deps(o)
        self._commit(o)
        return o

    def store(self, eng, t, out_ap, in_ap):
        self._dsem(t)
        o = _Op(eng, "st")
        o.tile = t
        o.reads = [t]
        o.dma_args = (out_ap, in_ap)
        o.cost = 0.08
        o.lat = 3.0
        self._deps(o)
        self._commit(o)
        return o

    def wait_all(self, eng, ops):
        o = _Op(eng, "w")
        o.preds = set(p.idx for p in ops if p is not None)
        o.cost = 0.01
        self._commit(o)
        return o

    def _schedule(self, win=None, noise=0.0, seed=0, quiet=False):
        ops = self.ops
        win = SCHED_WIN if win is None else win
        import random as _rnd
        rng = _rnd.Random(seed)
        n = len(ops)
        if NO_SCHED:
            order = {e: [] for e in self.ENGS}
            for o in ops:
                order[o.eng].append(o.idx)
            return order, list(range(n))
        for o in ops:
            o.succs = []
            o.npred = len(o.preds)
            o.start = None
            o.finish = None
        for o in ops:
            for p in o.preds:
                ops[p].succs.append(o.idx)
        bl = [0.0] * n
        for o in reversed(ops):
            m = 0.0
            for s_ in o.succs:
                v = bl[s_] + (0.06 if ops[s_].eng == o.eng else 0.25)
                if v > m:
                    m = v
            bl[o.idx] = m + o.cost + o.lat
        blp = [v * (1.0 + noise * (rng.random() - 0.5)) for v in bl] if noise > 0 else bl
        ready = {e: [] for e in self.ENGS}
        rtime = [0.0] * n
        for o in ops:
            if o.npred == 0:
                ready[o.eng].append(o.idx)
        free = {e: 0.0 for e in self.ENGS}
        order = {e: [] for e in self.ENGS}
        glob = []
        done = 0
        while done < n:
            best_e = None; best_t = None
            for e in self.ENGS:
                rl = ready[e]
                if not rl:
                    continue
                tmin = min(rtime[i] for i in rl)
                t_ = tmin if tmin > free[e] else free[e]
                if best_t is None or t_ < best_t:
                    best_t = t_; best_e = e
            e = best_e
            now = best_t
            cands = [i for i in ready[e] if rtime[i] <= now + win]
            if SCHED_PRIO == "bl":
                i = max(cands, key=lambda j: (blp[j], -j))
            else:
                i = min(cands, key=lambda j: (max(rtime[j], free[e]), j))
            stt = rtime[i] if rtime[i] > free[e] else free[e]
            ready[e].remove(i)
            o = ops[i]
            o.start = stt
            fin = stt + o.cost
            free[e] = fin
            o.finish = fin + o.lat
            order[e].append(i)
            glob.append(i)
            done += 1
            for s_ in o.succs:
                so = ops[s_]
                lat = 0.06 if so.eng == e else 0.25
                t2 = o.finish + lat
                if t2 > rtime[s_]:
                    rtime[s_] = t2
                so.npred -= 1
                if so.npred == 0:
                    ready[so.eng].append(s_)
        mk = max([0.0] + [o.finish for o in ops])
        self.last_mk = mk
        if quiet:
            return order, glob
        if self.log is not None and n > 1000:
            import collections as _c
            sp = _c.defaultdict(lambda: [1e18, 0.0])
            for o in ops:
                if o.label is None:
                    continue
                d = sp[o.label]
                d[0] = min(d[0], o.start); d[1] = max(d[1], o.finish)
            agg = _c.defaultdict(list)
            for (sname, ch), (a_, b_) in sp.items():
                agg[sname].append(b_ - a_)
            print("   stage spans:", ", ".join("%s=%.1f(max %.1f)" % (k, sum(v) / len(v), max(v)) for k, v in sorted(agg.items())))
        if self.log is not None and n > 1000:
            cp = [0.0] * n; par = [-1] * n
            for o in ops:
                best = 0.0; bp = -1
                for p in o.preds:
                    v = cp[p] + (0.06 if ops[p].eng == o.eng else 0.25)
                    if v > best:
                        best = v; bp = p
                cp[o.idx] = best + o.cost + o.lat; par[o.idx] = bp
            end = max(range(n), key=lambda i: cp[i])
            print("   critical path length %.1f" % cp[end])
            chain = []
            i = end
            while i >= 0:
                chain.append(i); i = par[i]
            chain.reverse()
            import collections
            agg = collections.OrderedDict()
            for i in chain:
                o = ops[i]
                key = (o.eng, (o.writes[0].name.rsplit("_", 1)[0] if o.writes else o.kind))
                agg[key] = agg.get(key, 0.0) + o.cost + o.lat
            top = sorted(agg.items(), key=lambda kv: -kv[1])[:12]
            print("   cp composition:", ", ".join("%s/%s=%.0f" % (k[0], k[1], v) for k, v in top))
        if self.log is not None:
            top = sorted([kv for kv in self.idle.items() if kv[0][0] in ('pe', 'dve', 'act')], key=lambda kv: -kv[1])[:16]
            for k, v in top:
                print("   idle %-5s waits %-5s %-10s -> %-10s %.1f" % (k[0], k[1], k[2], k[3], v))
            self.idle = {}
            busy = {e: sum(ops[i].cost for i in order[e]) for e in self.ENGS}
            print("EPOCH n=%d makespan=%.1f " % (n, mk) + " ".join("%s=%.0f" % (e, busy[e]) for e in self.ENGS))
        return order, glob

    def _emit_waits(self, eng, toks):
        need = {}
        for k, v in toks:
            if k == eng and eng == "pe":
                continue
            if self.waited[eng].get(k, 0) >= v:
                continue
            if need.get(k, 0) < v:
                need[k] = v
        e = self.emap[eng]
        for k, v in need.items():
            self.waited[eng][k] = v
            e.wait_ge(self.sems[k], v)
            if self.log is not None:
                self.log[eng].append(("w", k, v))

    def flush(self):
        if self.pend is not None:
            raise RuntimeError("open unsignaled group at flush")
        ops = self.ops
        if not ops:
            return
        if NO_SCHED or len(ops) < 200 or SCHED_TRIES <= 1:
            order, glob = self._schedule()
            best_mk = self.last_mk
        else:
            best = None
            cfgs = [(0.5, 0.0, 0), (0.0, 0.0, 0), (0.25, 0.0, 0), (1.0, 0.0, 0)]
            for s_ in range(SCHED_TRIES - len(cfgs)):
                cfgs.append((0.5 if s_ % 2 == 0 else 0.25, 0.04 + 0.02 * (s_ % 3), s_ + 1))
            for (w_, nz_, sd_) in cfgs:
                o_, g_ = self._schedule(win=w_, noise=nz_, seed=sd_, quiet=True)
                if best is None or self.last_mk < best[0]:
                    best = (self.last_mk, o_, g_, (w_, nz_, sd_))
            best_mk, order, glob, cfg = best
            if self.log is not None:
                print("   best schedule cfg", cfg, "makespan %.1f" % best_mk)
        self.sim_time += best_mk
        for e in self.ENGS:
            for i in order[e]:
                o = ops[i]
                if o.kind == "c":
                    self.cnt[e] += 1
                    o.tok = (e, self.cnt[e])
        for i in glob:
            o = ops[i]
            if o.kind in ("ld", "st"):
                o.tile.dma_cnt += 16
                o.tok = (o.tile.dma_sem, o.tile.dma_cnt)
        for e in self.ENGS:
            eng = self.emap[e]
            for i in order[e]:
                o = ops[i]
                hard_ = getattr(o, "hard", None)
                toks = [ops[p].tok for p in o.preds if ops[p].tok is not None
                        and not (hard_ is not None and e in SOFT_ENGS and ops[p].eng == e and p not in hard_)]
                self._emit_waits(e, toks)
                if o.kind == "c":
                    ins = None
                    for fn in o.fns:
                        ins = fn(eng)
                    ins.then_inc(self.sems[e], 1)
                    if self.log is not None:
                        self.log[e].append(("i", e, 1))
                elif o.kind in ("ld", "st"):
                    out_ap, in_ap = o.dma_args
                    eng.dma_start(out=out_ap, in_=in_ap).then_inc(self.sems[o.tile.dma_sem], 16)
                    if self.log is not None:
                        self.log[e].append(("i", o.tile.dma_sem, 16))
        self.ops = []
        for t in self.tiles:
            t.last_w = None
            t.readers = []
            t.alias_preds = set()

    def barrier(self):
        self.flush()
        toks = [(e, self.cnt[e]) for e in ("pe", "act", "dve", "pool") if self.cnt[e] > 0]
        for t in self.dma_tiles:
            if t.dma_cnt:
                toks.append((t.dma_sem, t.dma_cnt))
        for e in self.ENGS:
            self._emit_waits(e, toks)


import os as _os2
NEWTON_ENG = _os2.environ.get("KNEWTON", "pool")
NEWTON_CUR = [NEWTON_ENG]
NEWTON_A = _os2.environ.get("KNEWTONA", "pool")
NEWTON_B = _os2.environ.get("KNEWTONB", "pool")
EVAC_BC = _os2.environ.get("KEVAC", "act")
C_TT = _os2.environ.get("KCTT", "dve")
MT_ENG = _os2.environ.get("KMT", "dve")
BC_HILO = bool(int(_os2.environ.get("KBCHILO", "1")))
NSQ = int(_os2.environ.get("KNSQ", "4"))
C_B2 = _os2.environ.get("KCB2", "pool")
W_INFLIGHT = int(_os2.environ.get("KWINF", "3"))
PASS_BARRIER = bool(int(_os2.environ.get("KPASSBAR", "0")))
LNC_ENGS = _os2.environ.get("KLNC", "dve,pool,act").split(",")
C_PB = _os2.environ.get("KCPB", "act")


def pipeline(stage_fns, chunks):
    n = len(chunks)
    ns = len(stage_fns)
    for t in range(n + ns - 1):
        for s_ in reversed(range(ns)):
            i = t - s_
            if 0 <= i < n:
                stage_fns[s_](chunks[i], i)


def build(NCH=NCH_FULL, debug=False):
    nc = bass.Bass("TRN2", target_bir_lowering=False)
    NT = 2 * NCH * 128
    NM = NCH * 128

    def din(name, shape):
        return nc.dram_tensor(name, list(shape), F32, kind="ExternalInput").ap()

    xa = din("xa", [NT, D_MODEL])
    pa = din("pa", [NM, D_PLE])
    flag_d = din("flag", [128, 1])
    w_in = din("w_in", [D_MODEL, NE])
    w_out = din("w_out", [2048, D_MODEL])
    w_gate = din("w_gate", [D_MODEL, D_MODEL])
    w_ple = din("w_ple", [D_PLE, D_MODEL])
    g0fm_d = din("g0fm", [128, 8]); b0fm_d = din("b0fm", [128, 8])
    g0bc_d = din("g0bc", [128, 1024])
    b0row_d = din("b0row", [1, 1024]); boutrow_d = din("boutrow", [1, 1024])
    c4w_d = din("c4w", [128, 12, 4]); c4b_d = din("c4b", [128, 12])
    dtb_d = din("dtb", [128, 16]); alog_d = din("alog", [128, 16]); dfm_d = din("dfm", [128, 8])
    ngfm_d = din("ngfm", [128, 8])
    bglu_d = din("bglu", [1, 2048])
    c31w_d = din("c31w", [128, 8, 31]); c31b_d = din("c31b", [128, 8])
    clg_d = din("clg", [128, 8]); clb_d = din("clb", [128, 8])
    g1fm_d = din("g1fm", [128, 8]); b1fm_d = din("b1fm", [128, 8])
    g1bc_d = din("g1bc", [128, 1024]); b1bc_d = din("b1bc", [128, 1024])
    g2bc_d = din("g2bc", [128, 1024]); b2bc_d = din("b2bc", [128, 1024])
    out_d = nc.dram_tensor("out", [NM, D_MODEL], F32, kind="ExternalOutput").ap()
    if debug:
        dbg_ys = nc.dram_tensor("dbg_ys", [128, 8 * NM], F32, kind="ExternalOutput").ap()
        dbg_yc = nc.dram_tensor("dbg_yc", [128, 8 * NM], F32, kind="ExternalOutput").ap()

    with ExitStack() as st:
        fw = FW(nc, st)
        PS = [fw.ps(st, [128, 512], F32, f"bank{i}") for i in range(8)]

        def psb(i):
            return PS[i].ap[:, :].bitcast(BF16)

        R = "right"
        identf = fw.sb(st, [128, 128], F32, "identf", R)
        identb = fw.sb(st, [128, 128], BF16, "identb", R)
        onesf = fw.sb(st, [128, 128], F32, "onesf", R)
        onesb = fw.sb(st, [128, 128], BF16, "onesb", R)
        ones33 = fw.sb(st, [33, 128], BF16, "ones33", R)
        flag = fw.sb(st, [128, 1], F32, "flag", R)
        g0fm = fw.sb(st, [128, 8], F32, "g0fm", R)
        b0fm = fw.sb(st, [128, 8], F32, "b0fm", R)
        YC = fw.sb(st, [128, 8, NM], BF16, "YC", R)

        fw.op("pool", lambda e: e.memset(identf[:, :], 0.0), writes=[identf])
        fw.op("pool", lambda e: e.affine_select(out=identf[:, :], in_=identf[:, :], pattern=[[-1, 128]],
                                                compare_op=ALU.not_equal, fill=1.0, base=0, channel_multiplier=1),
              reads=[identf], writes=[identf])
        fw.op("pool", lambda e: e.tensor_copy(out=identb[:, :], in_=identf[:, :]), reads=[identf], writes=[identb])
        fw.op("pool", lambda e: e.memset(onesf[:, :], 1.0), writes=[onesf])
        fw.op("pool", lambda e: e.memset(onesb[:, :], 1.0), writes=[onesb])
        fw.op("pool", lambda e: e.memset(ones33[:, :], 1.0), writes=[ones33])
        fw.load("sp", flag, flag[:, :], flag_d[:, :])
        fw.load("sp", g0fm, g0fm[:, :], g0fm_d[:, :])
        fw.load("sp", b0fm, b0fm[:, :], b0fm_d[:, :])

        def rsqrt_chain(tiles, ve, n, iters=2, neng=None):
            neng = neng or NEWTON_CUR[0]
            ti, y, tb = tiles
            fw.op("dve", lambda e: e.tensor_single_scalar(out=ti[:, 0:n], in_=ve[:, 0:n].bitcast(I32), scalar=1,
                                                          op=ALU.arith_shift_right), reads=[ve], writes=[ti])
            fw.op("dve", lambda e: e.tensor_scalar(out=y[:, 0:n].bitcast(I32), in0=ti[:, 0:n], scalar1=-1.0,
                                                   scalar2=1597463007.0, op0=ALU.mult, op1=ALU.add),
                  reads=[ti], writes=[y])
            if neng == "act" and n == 1:
                nh = ti
                fw.op("act", lambda e: e.activation(out=nh[:, 0:1].bitcast(F32), in_=ve[:, 0:1], func=AF.Copy, scale=-0.5),
                      reads=[ve, ti], writes=[nh])
                for _ in range(iters):
                    fw.op("act", lambda e: e.activation(out=tb[:, 0:1], in_=y[:, 0:1], func=AF.Square), reads=[y], writes=[tb])
                    fw.op("act", lambda e: e.activation(out=tb[:, 0:1], in_=tb[:, 0:1], func=AF.Identity,
                                                        scale=nh[:, 0:1].bitcast(F32), bias=1.5), reads=[tb, nh], writes=[tb])
                    fw.op("act", lambda e: e.activation(out=y[:, 0:1], in_=y[:, 0:1], func=AF.Copy, scale=tb[:, 0:1]),
                          reads=[y, tb], writes=[y])
                return y
            for _ in range(iters):
                fw.op(neng, lambda e: e.tensor_tensor(out=tb[:, 0:n], in0=y[:, 0:n], in1=y[:, 0:n], op=ALU.mult),
                      reads=[y], writes=[tb])
                fw.op(neng, lambda e: e.tensor_tensor(out=tb[:, 0:n], in0=tb[:, 0:n], in1=ve[:, 0:n], op=ALU.mult),
                      reads=[tb, ve], writes=[tb])
                fw.op(neng, lambda e: e.tensor_scalar(out=tb[:, 0:n], in0=tb[:, 0:n], scalar1=-0.5, scalar2=1.5,
                                                      op0=ALU.mult, op1=ALU.add), reads=[tb], writes=[tb])
                fw.op(neng, lambda e: e.tensor_tensor(out=y[:, 0:n], in0=y[:, 0:n], in1=tb[:, 0:n], op=ALU.mult),
                      reads=[y, tb], writes=[y])
            return y

        class LNS:
            def __init__(self, stk, name, nsets=4):
                self.sets = []
                for i in range(nsets):
                    self.sets.append(dict(
                        stats=fw.sb(stk, [128, 2, 6], F32, name + "st"),
                        mv=fw.sb(stk, [128, 2], F32, name + "mv"),
                        ve=fw.sb(stk, [128, 1], F32, name + "ve"),
                        ti=fw.sb(stk, [128, 1], I32, name + "ti"),
                        y=fw.sb(stk, [128, 1], F32, name + "y"),
                        tb=fw.sb(stk, [128, 1], F32, name + "tb"),
                        nmr=fw.sb(stk, [128, 1], F32, name + "nmr")))
                self.i = 0

            def run(self, xt):
                s = self.sets[self.i % len(self.sets)]
                self.i += 1
                for c in range(2):
                    fw.op("dve", lambda e, c=c: e.bn_stats(out=s["stats"][:, c, :], in_=xt[:, c * 512:(c + 1) * 512]),
                          reads=[xt], writes=[s["stats"]])
                fw.op("dve", lambda e: e.bn_aggr(out=s["mv"][:, :], in_=s["stats"][:, :, :]),
                      reads=[s["stats"]], writes=[s["mv"]])
                fw.op("dve", lambda e: e.tensor_scalar_add(out=s["ve"][:, :], in0=s["mv"][:, 1:2], scalar1=LN_EPS),
                      reads=[s["mv"]], writes=[s["ve"]])
                engs = getattr(self, "engs", None)
                ne = engs[(self.i - 1) % len(engs)] if engs else NEWTON_CUR[0]
                y = rsqrt_chain((s["ti"], s["y"], s["tb"]), s["ve"], 1, neng=ne)
                if ne == "act":
                    fw.op("act", lambda e: e.activation(out=s["nmr"][:, :], in_=s["mv"][:, 0:1], func=AF.Copy, scale=y[:, 0:1]),
                          reads=[s["mv"], y], writes=[s["nmr"]])
                    fw.op("act", lambda e: e.activation(out=s["nmr"][:, :], in_=s["nmr"][:, :], func=AF.Copy, scale=-1.0),
                          reads=[s["nmr"]], writes=[s["nmr"]])
                elif ne == "dve":
                    fw.op("dve", lambda e: e.scalar_tensor_tensor(out=s["nmr"][:, :], in0=s["mv"][:, 0:1], scalar=-1.0,
                                                                   in1=y[:, :], op0=ALU.mult, op1=ALU.mult),
                          reads=[s["mv"], y], writes=[s["nmr"]])
                else:
                    fw.op(ne, lambda e: e.tensor_tensor(out=s["nmr"][:, :], in0=s["mv"][:, 0:1], in1=y[:, :], op=ALU.mult),
                          reads=[s["mv"], y], writes=[s["nmr"]])
                    fw.op(ne, lambda e: e.tensor_scalar_mul(out=s["nmr"][:, :], in0=s["nmr"][:, :], scalar1=-1.0),
                          reads=[s["nmr"]], writes=[s["nmr"]])
                return y, s["nmr"]

        ptc = [0]

        def next_pt():
            i = 2 + (ptc[0] % 2)
            ptc[0] += 1
            return i

        accc = [0]

        def next_acc():
            i = accc[0] % 2
            accc[0] += 1
            return i

        def front_end(tiles, lns, row0):
            xin, xnb, hT = tiles
            fw.load("sp", xin, xin[:, :], xa[row0:row0 + 128, :])
            rstd, nmr = lns.run(xin)
            fw.op("act", lambda e: e.activation(out=xnb[:, :], in_=xin[:, :], func=AF.Identity, bias=nmr[:, :],
                                                scale=rstd[:, :]), reads=[xin, nmr, rstd], writes=[xnb])
            pt = next_pt()
            for k in range(8):
                fw.op("pe", lambda e, k=k: e.transpose(out=psb(pt)[:, k * 128:(k + 1) * 128],
                                                       in_=xnb[:, k * 128:(k + 1) * 128], identity=identb[:, :]),
                      reads=[xnb, identb], writes=[PS[pt]], signal=(k == 7))
            evac_T(pt, hT, g0fm, b0fm)
            return hT

        def hsl(hT, k):
            return hT[:, k, :]

        evac_eng = ["dve"]

        def evac_T(pt, hT, gfm, bfm):
            for k in range(8):
                if evac_eng[0] == "act":
                    fw.op("act", lambda e, k=k: e.activation(out=hsl(hT, k), in_=psb(pt)[:, k * 128:(k + 1) * 128],
                                                             func=AF.Identity, scale=gfm[:, k:k + 1], bias=bfm[:, k:k + 1]),
                          reads=[PS[pt], gfm, bfm], writes=[hT])
                else:
                    fw.op("dve", lambda e, k=k: e.tensor_scalar(out=hsl(hT, k), in0=psb(pt)[:, k * 128:(k + 1) * 128],
                                                                scalar1=gfm[:, k:k + 1], scalar2=bfm[:, k:k + 1],
                                                                op0=ALU.mult, op1=ALU.add),
                          reads=[PS[pt], gfm, bfm], writes=[hT])

        def mk_hT(stk, name):
            return fw.sb(stk, [128, 8, 128], BF16, name)

        class WBlocks:
            def __init__(self, stk, name, src, K, blocks, inflight=W_INFLIGHT):
                self.blocks = []
                self.tiles = []
                ops_ = []
                for j, (c0, w) in enumerate(blocks):
                    t_ = fw.sb(stk, [128, K, w], BF16, name)
                    after = [ops_[j - inflight]] if j >= inflight else []
                    o_ = fw.load("pool", t_, t_[:, :, :], src[:, c0:c0 + w].rearrange("(k p) n -> p k n", p=128), after=after)
                    ops_.append(o_)
                    self.blocks.append((c0, w, t_))
                    self.tiles.append(t_)

            def get(self, k, col0, width):
                for (c0, w, t_) in self.blocks:
                    if c0 <= col0 and col0 + width <= c0 + w:
                        return t_, t_[:, k, col0 - c0:col0 - c0 + width]
                raise KeyError((col0, width))

        def proj(hT, W, col0, width, bank, pre=None):
            first = True
            if pre is not None:
                for (l_t, l_ap, r_t, r_ap) in pre:
                    fw.op("pe", lambda e, l_ap=l_ap, r_ap=r_ap, first=first: e.matmul(
                        PS[bank][:, 0:width], lhsT=l_ap, rhs=r_ap, start=first, stop=False),
                        reads=[l_t, r_t], writes=[PS[bank]], signal=False)
                    first = False
            for k in range(8):
                fw.op("pe", lambda e, k=k, first=first: e.matmul(
                    PS[bank][:, 0:width], lhsT=hsl(hT, k), rhs=W.get(k, col0, width)[1],
                    start=(first and k == 0), stop=(k == 7)),
                    reads=[hT, W.get(k, col0, width)[0]], writes=[PS[bank]], signal=(k == 7))

        def hilo_rows(stk, dst, src_d_list, ncols, CH=1024):
            f1 = fw.sb(stk, [33, CH], F32, "hl_f1")
            f2 = fw.sb(stk, [33, CH], F32, "hl_f2") if len(src_d_list) > 1 else None
            hf = fw.sb(stk, [33, CH], F32, "hl_hf")
            for c0 in range(0, ncols, CH):
                for r_ in (0, 32):
                    fw.load("sp", f1, f1[r_:r_ + 1, :], src_d_list[0][1][:, c0:c0 + CH])
                    if len(src_d_list) > 1:
                        fw.load("sp", f2, f2[r_:r_ + 1, :], src_d_list[1][1][:, c0:c0 + CH])
                if len(src_d_list) > 1:
                    for r_ in (0, 32):
                        fw.op("dve", lambda e, r_=r_: e.scalar_tensor_tensor(
                            out=f1[r_:r_ + 1, :], in0=f1[r_:r_ + 1, :], scalar=src_d_list[0][0], in1=f2[r_:r_ + 1, :],
                            op0=ALU.mult, op1=ALU.add), reads=[f1, f2], writes=[f1])
                fw.op("dve", lambda e, c0=c0: e.tensor_copy(out=dst[0:1, c0:c0 + CH], in_=f1[0:1, :]), reads=[f1, dst], writes=[dst])
                fw.op("dve", lambda e, c0=c0: e.tensor_copy(out=dst[32:33, c0:c0 + CH], in_=f1[32:33, :]), reads=[f1, dst], writes=[dst])
                fw.op("dve", lambda e, c0=c0: e.tensor_copy(out=hf[32:33, :], in_=dst[32:33, c0:c0 + CH]), reads=[dst], writes=[hf])
                fw.op("dve", lambda e, c0=c0: e.tensor_tensor(out=dst[32:33, c0:c0 + CH], in0=f1[32:33, :], in1=hf[32:33, :],
                                                              op=ALU.subtract), reads=[f1, hf, dst], writes=[dst])

        with ExitStack() as sbk:
            Wc = WBlocks(sbk, "Wc", w_in[:, N_SSD:NE], 8,
                         [(1024, 512), (0, 512), (1536, 512), (512, 512), (2048, 512), (2560, 512)])
            diag31 = fw.sb(sbk, [128, 8, 31, 128], BF16, "diag31")
            clg = fw.sb(sbk, [128, 8], F32, "clg")
            clb = fw.sb(sbk, [128, 8], F32, "clb")
            hclg = fw.sb(sbk, [128, 8], F32, "hclg")
            hclb = fw.sb(sbk, [128, 8], F32, "hclb")
            c31b = fw.sb(sbk, [128, 8], F32, "c31b")
            bglu2 = fw.sb(sbk, [33, 2048], BF16, "bglu2")
            fw.load("sp", clg, clg[:, :], clg_d[:, :])
            fw.load("sp", clb, clb[:, :], clb_d[:, :])
            fw.load("sp", c31b, c31b[:, :], c31b_d[:, :])
            fw.op("pool", lambda e: e.memset(bglu2[:, :], 0.0), writes=[bglu2])
            c31w = fw.sb(sbk, [128, 8, 31], F32, "c31w")
            w31s = fw.sb(sbk, [128, 8, 31], F32, "w31s")
            fw.load("sp", c31w, c31w[:, :, :], c31w_d[:, :, :])
            fw.op("dve", lambda e: e.tensor_scalar_mul(out=w31s[:, :, :], in0=c31w[:, :, :], scalar1=0.5),
                  reads=[c31w], writes=[w31s])
            for ct in range(8):
                eng = "pool" if ct in (0, 4) else "dve"
                fw.op(eng, lambda e, ct=ct: e.tensor_tensor(
                    out=diag31[:, ct, :, :],
                    in0=identf[:, :].unsqueeze(1).to_broadcast([128, 31, 128]),
                    in1=w31s[:, ct, :].unsqueeze(2).to_broadcast([128, 31, 128]),
                    op=ALU.mult), reads=[identf, w31s], writes=[diag31])
            hilo_rows(sbk, bglu2, [(1.0, bglu_d)], 2048)
            fw.op("dve", lambda e: e.tensor_scalar_mul(out=hclg[:, :], in0=clg[:, :], scalar1=0.5), reads=[clg], writes=[hclg])
            fw.op("dve", lambda e: e.tensor_scalar_mul(out=hclb[:, :], in0=clb[:, :], scalar1=0.5), reads=[clb], writes=[hclb])

            lnsB = LNS(sbk, "lnB")
            xins = [fw.sb(sbk, [128, 1024], F32, "xinB") for _ in range(2)]
            xnbs = [fw.sb(sbk, [128, 1024], BF16, "xnbB") for _ in range(2)]
            hTs = [mk_hT(sbk, "hTB") for _ in range(2)]
            tgs = [fw.sb(sbk, [128, 512], F32, "tg") for _ in range(2)]
            u2s = [fw.sb(sbk, [128, 1024], BF16, "u2") for _ in range(2)]
            _gcg = fw.sb(sbk, [128, 1024], BF16, "gcg")
            gcgs = [_gcg, _gcg]
            u_fms = [fw.sb(sbk, [128, 8, 158], BF16, "u_fm") for _ in range(2)]
            gcg_fms = [fw.sb(sbk, [128, 8, 128], BF16, "gcg_fm") for _ in range(2)]
            cvs = [fw.sb(sbk, [128, 8, 128], F32, "cv") for _ in range(2)]
            sqs = [fw.sb(sbk, [128, 128], F32, "sqs") for _ in range(NSQ)]
            _mean = fw.sb(sbk, [128, 128], F32, "mean")
            mean_l = [_mean, _mean]
            _vev = fw.sb(sbk, [128, 128], F32, "vev")
            vev_l = [_vev, _vev]
            _tiv = fw.sb(sbk, [128, 128], I32, "tiv")
            tiv_l = [_tiv, _tiv]
            _yv = fw.sb(sbk, [128, 128], F32, "yv")
            yv_l = [_yv, _yv]
            _tbv = fw.sb(sbk, [128, 128], F32, "tbv")
            tbv_l = [_tbv, _tbv]
            for t_ in u_fms:
                fw.op("pool", lambda e, t_=t_: e.memset(t_[:, :, :], 0.0), writes=[t_])
            sqc = [0]

            def next_sq():
                t_ = sqs[sqc[0] % NSQ]
                sqc[0] += 1
                return t_

            def B_s0(m, i):
                c = NCH + m
                front_end((xins[i % 2], xnbs[i % 2], hTs[i % 2]), lnsB, c * 128)

            def B_s1(m, i):
                hT = hTs[i % 2]
                u2 = u2s[i % 2]
                gcg = gcgs[i % 2]
                u_fm = u_fms[i % 2]
                u_nx = u_fms[(i + 1) % 2]
                gcg_fm = gcg_fms[i % 2]
                for blk in range(2):
                    def bias_pre(col):
                        return [(ones33, ones33[:, :], bglu2, bglu2[:, col:col + 512])]
                    bg = next_acc()
                    proj(hT, Wc, 1024 + blk * 512, 512, bg, pre=bias_pre(1024 + blk * 512))
                    tgt = tgs[blk]
                    fw.op("act", lambda e, bg=bg, tgt=tgt: e.activation(out=tgt[:, :], in_=PS[bg][:, :], func=AF.Tanh,
                                                                        scale=0.5), reads=[PS[bg]], writes=[tgt])
                    bv = next_acc()
                    proj(hT, Wc, blk * 512, 512, bv, pre=bias_pre(blk * 512))
                    fw.op("dve", lambda e, bv=bv, tgt=tgt, blk=blk: e.scalar_tensor_tensor(
                        out=u2[:, blk * 512:(blk + 1) * 512], in0=tgt[:, :], scalar=1.0, in1=PS[bv][:, :],
                        op0=ALU.add, op1=ALU.mult), reads=[tgt, PS[bv]], writes=[u2])
                ptu = next_pt()
                for k in range(8):
                    fw.op("pe", lambda e, k=k: e.transpose(out=psb(ptu)[:, k * 128:(k + 1) * 128],
                                                           in_=u2[:, k * 128:(k + 1) * 128], identity=identb[:, :]),
                          reads=[u2, identb], writes=[PS[ptu]], signal=(k == 7))
                fw.op("act", lambda e: e.activation(out=u_fm[:, :, 30:158],
                                                    in_=psb(ptu)[:, :].rearrange("p (a b) -> p a b", a=8), func=AF.Copy),
                      reads=[PS[ptu]], writes=[u_fm])
                fw.op("pool", lambda e: e.tensor_copy(out=u_nx[:, :, 0:30], in_=u_fm[:, :, 128:158]),
                      reads=[u_fm, u_nx], writes=[u_nx])
                if m == -1:
                    fw.op("pool", lambda e: e.tensor_scalar_mul(out=u_nx[:, :, 0:30], in0=u_nx[:, :, 0:30],
                                                                scalar1=flag[:, 0:1]), reads=[u_nx, flag], writes=[u_nx])
                    return
                for blk in range(2):
                    bc_ = next_acc()
                    proj(hT, Wc, 2048 + blk * 512, 512, bc_)
                    tgt = tgs[blk]
                    fw.op("act", lambda e, bc_=bc_, tgt=tgt: e.activation(out=tgt[:, :], in_=PS[bc_][:, :],
                                                                          func=AF.Tanh, scale=0.5),
                          reads=[PS[bc_]], writes=[tgt])
                    fw.op("dve", lambda e, bc_=bc_, tgt=tgt, blk=blk: e.scalar_tensor_tensor(
                        out=gcg[:, blk * 512:(blk + 1) * 512], in0=tgt[:, :], scalar=1.0, in1=PS[bc_][:, :],
                        op0=ALU.add, op1=ALU.mult), reads=[tgt, PS[bc_]], writes=[gcg])
                ptg = next_pt()
                for k in range(8):
                    fw.op("pe", lambda e, k=k: e.transpose(out=psb(ptg)[:, k * 128:(k + 1) * 128],
                                                           in_=gcg[:, k * 128:(k + 1) * 128], identity=identb[:, :]),
                          reads=[gcg, identb], writes=[PS[ptg]], signal=(k == 7))
                fw.op("act", lambda e: e.activation(out=gcg_fm[:, :, :],
                                                    in_=psb(ptg)[:, :].rearrange("p (a b) -> p a b", a=8), func=AF.Copy,
                                                    scale=0.25),
                      reads=[PS[ptg]], writes=[gcg_fm])

            def B_s2(m, i):
                if m == -1:
                    return
                u_fm = u_fms[i % 2]
                cv = cvs[i % 2]
                for half in range(2):
                    bk = 4 + half
                    for j in range(4):
                        ct = half * 4 + j
                        for k in range(31):
                            fw.op("pe", lambda e, j=j, ct=ct, k=k, bk=bk: e.matmul(
                                PS[bk][:, j * 128:(j + 1) * 128], lhsT=diag31[:, ct, k, :], rhs=u_fm[:, ct, k:k + 128],
                                start=(k == 0), stop=(k == 30)), reads=[diag31, u_fm], writes=[PS[bk]],
                                signal=(j == 3 and k == 30))
                    for j in range(4):
                        ct = half * 4 + j
                        if j % 2 == 0:
                            fw.op("act", lambda e, j=j, ct=ct, bk=bk: e.activation(
                                out=cv[:, ct, :], in_=PS[bk][:, j * 128:(j + 1) * 128], func=AF.Identity,
                                bias=c31b[:, ct:ct + 1]), reads=[PS[bk], c31b], writes=[cv])
                        else:
                            fw.op("dve", lambda e, j=j, ct=ct, bk=bk: e.tensor_scalar_add(
                                out=cv[:, ct, :], in0=PS[bk][:, j * 128:(j + 1) * 128], scalar1=c31b[:, ct:ct + 1]),
                                reads=[PS[bk], c31b], writes=[cv])

            def B_s3(m, i):
                if m == -1:
                    return
                cv = cvs[i % 2]
                gcg_fm = gcg_fms[i % 2]
                mean = mean_l[i % 2]; vev = vev_l[i % 2]; tiv = tiv_l[i % 2]; yv = yv_l[i % 2]; tbv = tbv_l[i % 2]
                msq = tbv
                pst = 6
                for ct in range(8):
                    fw.op("pe", lambda e, ct=ct: e.matmul(PS[pst][:, 0:128], lhsT=onesf[:, :], rhs=cv[:, ct, :],
                                                          start=(ct == 0), stop=(ct == 7)),
                          reads=[onesf, cv], writes=[PS[pst]], signal=(ct == 7))
                for ct in range(8):
                    sq_ = next_sq()
                    fw.op("pool", lambda e, ct=ct, sq_=sq_: e.tensor_tensor(out=sq_[:, :], in0=cv[:, ct, :], in1=cv[:, ct, :],
                                                                            op=ALU.mult), reads=[cv], writes=[sq_])
                    fw.op("pe", lambda e, ct=ct, sq_=sq_: e.matmul(PS[pst][:, 128:256], lhsT=onesf[:, :], rhs=sq_[:, :],
                                                                   start=(ct == 0), stop=(ct == 7)),
                          reads=[onesf, sq_], writes=[PS[pst]], signal=True)
                fw.op("dve", lambda e: e.tensor_scalar_mul(out=mean[:, :], in0=PS[pst][:, 0:128], scalar1=1.0 / 1024.0),
                      reads=[PS[pst]], writes=[mean])
                fw.op("dve", lambda e: e.tensor_tensor(out=msq[:, :], in0=mean[:, :], in1=mean[:, :], op=ALU.mult),
                      reads=[mean], writes=[msq])
                fw.op("dve", lambda e: e.scalar_tensor_tensor(out=vev[:, :], in0=PS[pst][:, 128:256], scalar=1.0 / 1024.0,
                                                               in1=msq[:, :], op0=ALU.mult, op1=ALU.subtract),
                      reads=[PS[pst], msq], writes=[vev])
                fw.op("dve", lambda e: e.tensor_scalar(out=vev[:, :], in0=vev[:, :], scalar1=0.0, scalar2=LN_EPS,
                                                       op0=ALU.max, op1=ALU.add), reads=[vev], writes=[vev])
                rv = rsqrt_chain((tiv, yv, tbv), vev, 128)
                fw.op("dve", lambda e: e.tensor_tensor(out=cv[:, :, :], in0=cv[:, :, :],
                                                       in1=mean[:, :].unsqueeze(1).to_broadcast([128, 8, 128]),
                                                       op=ALU.subtract), reads=[cv, mean], writes=[cv])
                fw.op("pool", lambda e: e.tensor_tensor(out=cv[:, :, :], in0=cv[:, :, :],
                                                        in1=rv[:, :].unsqueeze(1).to_broadcast([128, 8, 128]),
                                                        op=ALU.mult), reads=[cv, rv], writes=[cv])
                for ct in range(8):
                    tl_ = next_sq()
                    fw.op("act", lambda e, ct=ct, tl_=tl_: e.activation(out=tl_[:, :], in_=cv[:, ct, :], func=AF.Tanh,
                                                                        scale=hclg[:, ct:ct + 1], bias=hclb[:, ct:ct + 1]),
                          reads=[cv, hclg, hclb], writes=[tl_])
                    fw.op("dve", lambda e, ct=ct: e.tensor_scalar(out=cv[:, ct, :], in0=cv[:, ct, :],
                                                                  scalar1=clg[:, ct:ct + 1], scalar2=clb[:, ct:ct + 1],
                                                                  op0=ALU.mult, op1=ALU.add),
                          reads=[cv, clg, clb], writes=[cv])
                    fw.op("dve", lambda e, ct=ct, tl_=tl_: e.scalar_tensor_tensor(
                        out=cv[:, ct, :], in0=tl_[:, :], scalar=1.0, in1=cv[:, ct, :], op0=ALU.add, op1=ALU.mult),
                        reads=[tl_, cv], writes=[cv])
                fw.op("pool", lambda e, m=m: e.tensor_tensor(out=YC[:, :, m * 128:(m + 1) * 128], in0=cv[:, :, :],
                                                             in1=gcg_fm[:, :, :], op=ALU.mult),
                      reads=[cv, gcg_fm], writes=[YC])

            evac_eng[0] = EVAC_BC
            NEWTON_CUR[0] = NEWTON_B
            pipeline([B_s0, B_s1, B_s2, B_s3], list(range(-1, NCH)))
            if PASS_BARRIER:
                fw.barrier()

        YS = fw.sb(st, [128, 8, NM], BF16, "YS", R)
        with ExitStack() as sa:
            Wssd = WBlocks(sa, "Wssd", w_in[:, 0:N_SSD], 8,
                           [(0, 512), (512, 512), (2048, 528), (1024, 512), (1536, 512)])
            diag4 = fw.sb(sa, [128, 12, 5, 128], BF16, "diag4")
            dtb = fw.sb(sa, [128, 16], F32, "dtb")
            abc = fw.sb(sa, [128, 16], F32, "abc")
            dDhi = fw.sb(sa, [128, 8, 128], BF16, "dDhi")
            dDlo = fw.sb(sa, [128, 8, 128], BF16, "dDlo")
            U = fw.sb(sa, [128, 128], F32, "U")
            NEGM = fw.sb(sa, [128, 4, 128], BF16, "NEGM")
            fw.load("sp", dtb, dtb[:, :], dtb_d[:, :])
            fw.load("sp", abc, abc[:, :], alog_d[:, :])
            if True:
                stmp = sa
                c4w = fw.sb(stmp, [128, 12, 4], F32, "c4w")
                c4b = fw.sb(stmp, [128, 12], F32, "c4b")
                w4s = fw.sb(stmp, [128, 12, 5], F32, "w4s")
                dfm = fw.sb(stmp, [128, 8], F32, "dfm")
                dhi_b = fw.sb(stmp, [128, 8], BF16, "dhi_b")
                dhi_f = fw.sb(stmp, [128, 8], F32, "dhi_f")
                dlo_f = fw.sb(stmp, [128, 8], F32, "dlo_f")
                negf = fw.sb(stmp, [128, 128], F32, "negf")
                fw.load("sp", c4w, c4w[:, :, :], c4w_d[:, :, :])
                fw.load("sp", c4b, c4b[:, :], c4b_d[:, :])
                fw.load("sp", dfm, dfm[:, :], dfm_d[:, :])
                fw.op("dve", lambda e: e.tensor_scalar_mul(out=w4s[:, :, 0:4], in0=c4w[:, :, :], scalar1=0.5),
                      reads=[c4w], writes=[w4s])
                fw.op("dve", lambda e: e.tensor_scalar_mul(out=w4s[:, :, 4:5], in0=c4b[:, :].unsqueeze(2), scalar1=0.5),
                      reads=[c4b, w4s], writes=[w4s])
                fw.op("dve", lambda e: e.tensor_tensor(
                    out=diag4[:, :, :, :].rearrange("p a b c -> p (a b) c"),
                    in0=identf[:, :].unsqueeze(1).to_broadcast([128, 60, 128]),
                    in1=w4s[:, :, :].rearrange("p a b -> p (a b)").unsqueeze(2).to_broadcast([128, 60, 128]),
                    op=ALU.mult), reads=[identf, w4s], writes=[diag4])
                fw.op("act", lambda e: e.activation(out=abc[:, :], in_=abc[:, :], func=AF.Exp), reads=[abc], writes=[abc])
                fw.op("dve", lambda e: e.tensor_scalar_mul(out=abc[:, :], in0=abc[:, :], scalar1=-1.0), reads=[abc], writes=[abc])
                fw.op("dve", lambda e: e.tensor_copy(out=dhi_b[:, :], in_=dfm[:, :]), reads=[dfm], writes=[dhi_b])
                fw.op("dve", lambda e: e.tensor_copy(out=dhi_f[:, :], in_=dhi_b[:, :]), reads=[dhi_b], writes=[dhi_f])
                fw.op("dve", lambda e: e.tensor_tensor(out=dlo_f[:, :], in0=dfm[:, :], in1=dhi_f[:, :], op=ALU.subtract),
                      reads=[dfm, dhi_f], writes=[dlo_f])
                fw.op("pool", lambda e: e.tensor_tensor(
                    out=dDhi[:, :, :], in0=identf[:, :].unsqueeze(1).to_broadcast([128, 8, 128]),
                    in1=dhi_f[:, :].unsqueeze(2).to_broadcast([128, 8, 128]), op=ALU.mult),
                    reads=[identf, dhi_f], writes=[dDhi])
                fw.op("pool", lambda e: e.tensor_tensor(
                    out=dDlo[:, :, :], in0=identf[:, :].unsqueeze(1).to_broadcast([128, 8, 128]),
                    in1=dlo_f[:, :].unsqueeze(2).to_broadcast([128, 8, 128]), op=ALU.mult),
                    reads=[identf, dlo_f], writes=[dDlo])
                fw.op("pool", lambda e: e.memset(U[:, :], 1.0), writes=[U])
                fw.op("pool", lambda e: e.affine_select(out=U[:, :], in_=U[:, :], pattern=[[1, 128]], compare_op=ALU.is_ge,
                                                        fill=0.0, base=0, channel_multiplier=-1), reads=[U], writes=[U])
                fw.op("pool", lambda e: e.memset(negf[:, :], 0.0), writes=[negf])
                fw.op("pool", lambda e: e.affine_select(out=negf[:, :], in_=negf[:, :], pattern=[[1, 128]],
                                                        compare_op=ALU.is_ge, fill=-30000.0, base=0, channel_multiplier=-1),
                      reads=[negf], writes=[negf])
                fw.op("pool", lambda e: e.tensor_copy(out=NEGM[:, :, :],
                                                      in_=negf[:, :].unsqueeze(1).to_broadcast([128, 4, 128])),
                      reads=[negf], writes=[NEGM])

            lnsA = LNS(sa, "lnA")
            xins = [fw.sb(sa, [128, 1024], F32, "xin") for _ in range(2)]
            _xnb = fw.sb(sa, [128, 1024], BF16, "xnb")
            xnbs = [_xnb, _xnb]
            hTs = [mk_hT(sa, "hT") for _ in range(2)]
            _stg = [fw.sb(sa, [128, 512], BF16, "stage") for _ in range(3)]
            stages = [_stg, _stg]
            xbcs = [fw.sb(sa, [128, 12, 131], BF16, "xbc_fm") for _ in range(2)]
            th = [fw.sb(sa, [128, 512], F32, "th") for _ in range(2)]
            lq = [fw.sb(sa, [128, 512], F32, "lq") for _ in range(2)]
            x_cs = [fw.sb(sa, [128, 12, 128], BF16, "x_c") for _ in range(2)]
            gzs = [fw.sb(sa, [128, 1024], BF16, "gz") for _ in range(2)]
            xdts = [fw.sb(sa, [128, 1024], BF16, "xdt") for _ in range(2)]
            xdds = [fw.sb(sa, [128, 1024], BF16, "xdd") for _ in range(2)]
            B_tms = [fw.sb(sa, [128, 256], BF16, "B_tm") for _ in range(2)]

            def small2(name):
                return [fw.sb(sa, [128, 16], F32, name) for _ in range(2)]
            e1s = small2("e1"); z1s = small2("z1"); w1s = small2("w1"); p1s = small2("p1")
            vts = small2("vt"); a1s = small2("a1"); dtts = small2("dtt"); adts = small2("adt")
            acss = small2("acs"); nacss = small2("nacs"); dds = small2("dd"); dstates = small2("dstate")
            cdts = small2("cdt"); expacss = small2("expacs")
            cbs_t = fw.sb(sa, [128, 256], F32, "cbs")
            adt_his = [fw.sb(sa, [128, 16], BF16, "adt_hi") for _ in range(2)]
            adt_los = [fw.sb(sa, [128, 16], BF16, "adt_lo") for _ in range(2)]
            adt_hfs = [fw.sb(sa, [128, 16], F32, "adt_hf") for _ in range(2)]
            Ub = fw.sb(sa, [128, 128], BF16, "Ub")
            fw.op("pool", lambda e: e.tensor_copy(out=Ub[:, :], in_=U[:, :]), reads=[U], writes=[Ub])
            MT_l = [fw.sb(sa, [128, 16, 128], BF16, "MT") for _ in range(2)]
            S = fw.sb(sa, [128, 1024], F32, "S")
            Sbf = fw.sb(sa, [128, 1024], BF16, "Sbf")
            yt_l = [fw.sb(sa, [128, 1024], F32, "yt") for _ in range(2)]
            ss_l = [fw.sb(sa, [128, 2], F32, "ss") for _ in range(2)]
            ve2_l = [fw.sb(sa, [128, 2], F32, "ve2") for _ in range(2)]
            ti2_l = [fw.sb(sa, [128, 2], I32, "ti2") for _ in range(2)]
            y2_l = [fw.sb(sa, [128, 2], F32, "y2") for _ in range(2)]
            tb2_l = [fw.sb(sa, [128, 2], F32, "tb2") for _ in range(2)]
            _yn = fw.sb(sa, [128, 1024], BF16, "yn")
            yn_l = [_yn, _yn]

            fw.op("pool", lambda e: e.memset(S[:, :], 0.0), writes=[S])
            fw.op("pool", lambda e: e.memset(Sbf[:, :], 0.0), writes=[Sbf])
            for t_ in xbcs:
                fw.op("pool", lambda e, t_=t_: e.memset(t_[:, :, :], 0.0), writes=[t_])

            def A_s0(c, i):
                front_end((xins[i % 2], xnbs[i % 2], hTs[i % 2]), lnsA, c * 128)

            def A_s1(c, i):
                main = c >= NCH
                hT = hTs[i % 2]
                stage = stages[i % 2]
                xbc = xbcs[i % 2]
                xbc_nx = xbcs[(i + 1) % 2]
                vt = vts[i % 2]
                gz = gzs[i % 2]
                for bi, col0 in enumerate((0, 512, 2048)):
                    bk = next_acc()
                    proj(hT, Wssd, col0, 512, bk)
                    if bi % 2 == 0:
                        fw.op("act", lambda e, bk=bk, bi=bi: e.activation(out=stage[bi][:, :],
                                                                          in_=PS[bk][:, :], func=AF.Copy),
                              reads=[PS[bk]], writes=[stage[bi]])
                    else:
                        fw.op("dve", lambda e, bk=bk, bi=bi: e.tensor_copy(out=stage[bi][:, :],
                                                                           in_=PS[bk][:, :]),
                              reads=[PS[bk]], writes=[stage[bi]])
                bk = next_acc()
                proj(hT, Wssd, 2560, 16, bk)
                fw.op("dve", lambda e, bk=bk: e.tensor_tensor(out=vt[:, :], in0=PS[bk][:, 0:16], in1=dtb[:, :], op=ALU.add),
                      reads=[PS[bk], dtb], writes=[vt])
                if main:
                    for zi in range(2):
                        bk = next_acc()
                        proj(hT, Wssd, 1024 + zi * 512, 512, bk)
                        tht = th[zi]
                        fw.op("act", lambda e, bk=bk, tht=tht: e.activation(out=tht[:, :], in_=PS[bk][:, :], func=AF.Tanh,
                                                                            scale=0.5), reads=[PS[bk]], writes=[tht])
                        fw.op("dve", lambda e, bk=bk, tht=tht, zi=zi: e.scalar_tensor_tensor(
                            out=gz[:, zi * 512:(zi + 1) * 512], in0=tht[:, :], scalar=1.0, in1=PS[bk][:, :],
                            op0=ALU.add, op1=ALU.mult), reads=[tht, PS[bk]], writes=[gz])
                for grp, (ct0, nct) in enumerate(((0, 8), (8, 4))):
                    pt = next_pt()
                    for j in range(nct):
                        ct = ct0 + j
                        fw.op("pe", lambda e, j=j, ct=ct, pt=pt: e.transpose(
                            out=psb(pt)[:, j * 128:(j + 1) * 128], in_=stage[ct // 4][:, (ct % 4) * 128:(ct % 4 + 1) * 128],
                            identity=identb[:, :]), reads=[stage[ct // 4], identb], writes=[PS[pt]], signal=(j == nct - 1))
                    eng = "act" if grp == 0 else "dve"
                    if eng == "act":
                        fw.op("act", lambda e, pt=pt, ct0=ct0, nct=nct: e.activation(
                            out=xbc[:, ct0:ct0 + nct, 3:131],
                            in_=psb(pt)[:, 0:nct * 128].rearrange("p (a b) -> p a b", a=nct), func=AF.Copy),
                            reads=[PS[pt]], writes=[xbc])
                    else:
                        fw.op("dve", lambda e, pt=pt, ct0=ct0, nct=nct: e.tensor_copy(
                            out=xbc[:, ct0:ct0 + nct, 3:131],
                            in_=psb(pt)[:, 0:nct * 128].rearrange("p (a b) -> p a b", a=nct)),
                            reads=[PS[pt]], writes=[xbc])
                fw.op("pool", lambda e: e.tensor_copy(out=xbc_nx[:, :, 0:3], in_=xbc[:, :, 128:131]),
                      reads=[xbc, xbc_nx], writes=[xbc_nx])
                if c == NCH - 1:
                    fw.op("pool", lambda e: e.tensor_scalar_mul(out=xbc_nx[:, :, 0:3], in0=xbc_nx[:, :, 0:3],
                                                                scalar1=flag[:, 0:1]), reads=[xbc_nx, flag], writes=[xbc_nx])

            def A_s2(c, i):
                main = c >= NCH
                xbc = xbcs[i % 2]
                x_c = x_cs[i % 2]
                vt = vts[i % 2]; a1 = a1s[i % 2]; dtt = dtts[i % 2]; adt = adts[i % 2]
                acs = acss[i % 2]; nacs = nacss[i % 2]; dd = dds[i % 2]; dstate = dstates[i % 2]
                cdt = cdts[i % 2]; expacs = expacss[i % 2]
                xdt = xdts[i % 2]; xdd = xdds[i % 2]; B_tm = B_tms[i % 2]
                e1 = e1s[i % 2]; z1 = z1s[i % 2]; w1 = w1s[i % 2]; p1 = p1s[i % 2]
                fw.op("act", lambda e: e.activation(out=a1[:, :], in_=vt[:, :], func=AF.Abs), reads=[vt], writes=[a1])
                fw.op("act", lambda e: e.activation(out=e1[:, :], in_=a1[:, :], func=AF.Exp, scale=-1.0), reads=[a1], writes=[e1])
                fw.op("pool", lambda e: e.tensor_scalar_add(out=z1[:, :], in0=e1[:, :], scalar1=2.0), reads=[e1], writes=[z1])
                fw.op("dve", lambda e: e.reciprocal(out=z1[:, :], in_=z1[:, :]), reads=[z1], writes=[z1])
                fw.op("pool", lambda e: e.tensor_tensor(out=z1[:, :], in0=e1[:, :], in1=z1[:, :], op=ALU.mult), reads=[e1, z1], writes=[z1])
                fw.op("pool", lambda e: e.tensor_tensor(out=w1[:, :], in0=z1[:, :], in1=z1[:, :], op=ALU.mult), reads=[z1], writes=[w1])
                fw.op("pool", lambda e: e.tensor_scalar(out=p1[:, :], in0=w1[:, :], scalar1=1.0 / 11.0, scalar2=1.0 / 9.0,
                                                        op0=ALU.mult, op1=ALU.add), reads=[w1], writes=[p1])
                for cf in (1.0 / 7.0, 1.0 / 5.0, 1.0 / 3.0, 1.0):
                    fw.op("pool", lambda e: e.tensor_tensor(out=p1[:, :], in0=p1[:, :], in1=w1[:, :], op=ALU.mult),
                          reads=[p1, w1], writes=[p1])
                    fw.op("pool", lambda e, cf=cf: e.tensor_scalar_add(out=p1[:, :], in0=p1[:, :], scalar1=cf),
                          reads=[p1], writes=[p1])
                fw.op("pool", lambda e: e.tensor_tensor(out=p1[:, :], in0=p1[:, :], in1=z1[:, :], op=ALU.mult),
                      reads=[p1, z1], writes=[p1])
                fw.op("dve", lambda e: e.tensor_scalar_max(out=dtt[:, :], in0=vt[:, :], scalar1=0.0), reads=[vt], writes=[dtt])
                fw.op("dve", lambda e: e.scalar_tensor_tensor(out=dtt[:, :], in0=p1[:, :], scalar=2.0, in1=dtt[:, :],
                                                               op0=ALU.mult, op1=ALU.add), reads=[p1, dtt], writes=[dtt])
                fw.op("pool", lambda e: e.tensor_tensor(out=adt[:, :], in0=dtt[:, :], in1=abc[:, :], op=ALU.mult),
                      reads=[dtt, abc], writes=[adt])
                if main and BC_HILO:
                    adt_hi = adt_his[i % 2]; adt_lo = adt_los[i % 2]; adt_hf = adt_hfs[i % 2]
                    fw.op("pool", lambda e: e.tensor_copy(out=adt_hi[:, :], in_=adt[:, :]), reads=[adt], writes=[adt_hi])
                    fw.op("pool", lambda e: e.tensor_copy(out=adt_hf[:, :], in_=adt_hi[:, :]), reads=[adt_hi], writes=[adt_hf])
                    fw.op("pool", lambda e: e.tensor_tensor(out=adt_lo[:, :], in0=adt[:, :], in1=adt_hf[:, :], op=ALU.subtract),
                          reads=[adt, adt_hf], writes=[adt_lo])
                nct_conv = 12 if main else 10
                for grp in range(3):
                    cts = [ct for ct in range(grp * 4, grp * 4 + 4) if ct < nct_conv]
                    bk = next_acc()
                    for j, ct in enumerate(cts):
                        for k in range(4):
                            fw.op("pe", lambda e, j=j, ct=ct, k=k, bk=bk: e.matmul(
                                PS[bk][:, j * 128:(j + 1) * 128], lhsT=diag4[:, ct, k, :], rhs=xbc[:, ct, k:k + 128],
                                start=(k == 0), stop=False), reads=[diag4, xbc], writes=[PS[bk]], signal=False)
                        fw.op("pe", lambda e, j=j, ct=ct, bk=bk: e.matmul(
                            PS[bk][:, j * 128:(j + 1) * 128], lhsT=diag4[:, ct, 4, :], rhs=onesb[:, :],
                            start=False, stop=True), reads=[diag4, onesb], writes=[PS[bk]], signal=(j == len(cts) - 1))
                    w = len(cts) * 128
                    tht = th[grp % 2]
                    fw.op("act", lambda e, bk=bk, tht=tht, w=w: e.activation(out=tht[:, 0:w], in_=PS[bk][:, 0:w], func=AF.Tanh),
                          reads=[PS[bk]], writes=[tht])
                    fw.op("dve", lambda e, bk=bk, tht=tht, w=w, cts=cts: e.scalar_tensor_tensor(
                        out=x_c[:, cts[0]:cts[0] + len(cts), :].rearrange("p a b -> p (a b)"),
                        in0=tht[:, 0:w], scalar=1.0, in1=PS[bk][:, 0:w], op0=ALU.add, op1=ALU.mult),
                        reads=[tht, PS[bk]], writes=[x_c])
                pq = 4
                fw.op("pe", lambda e: e.matmul(PS[pq][:, 0:16], lhsT=U[:, :], rhs=adt[:, :], start=True, stop=True),
                      reads=[U, adt], writes=[PS[pq]], signal=False)
                fw.op("pe", lambda e: e.matmul(PS[pq][:, 16:32], lhsT=onesf[:, :], rhs=adt[:, :], start=True, stop=True),
                      reads=[onesf, adt], writes=[PS[pq]])
                fw.op("dve", lambda e: e.tensor_copy(out=acs[:, :], in_=PS[pq][:, 0:16]), reads=[PS[pq]], writes=[acs])
                fw.op("dve", lambda e: e.tensor_tensor(out=dd[:, :], in0=PS[pq][:, 16:32], in1=acs[:, :], op=ALU.subtract),
                      reads=[PS[pq], acs], writes=[dd])
                fw.op("act", lambda e: e.activation(out=dstate[:, :], in_=dd[:, :], func=AF.Exp), reads=[dd], writes=[dstate])
                fw.op("act", lambda e: e.activation(out=cdt[:, :], in_=PS[pq][:, 16:32], func=AF.Exp), reads=[PS[pq]], writes=[cdt])
                if main:
                    fw.op("pool", lambda e: e.tensor_scalar_mul(out=nacs[:, :], in0=acs[:, :], scalar1=-1.0),
                          reads=[acs], writes=[nacs])
                    fw.op("act", lambda e: e.activation(out=expacs[:, :], in_=acs[:, :], func=AF.Exp), reads=[acs], writes=[expacs])
                ptx = next_pt()
                for ct in range(8):
                    fw.op("pe", lambda e, ct=ct: e.transpose(out=psb(ptx)[:, ct * 128:(ct + 1) * 128], in_=x_c[:, ct, :],
                                                             identity=identb[:, :]),
                          reads=[x_c, identb], writes=[PS[ptx]], signal=(ct == 7))
                fw.op("dve", lambda e: e.tensor_tensor(
                    out=xdt[:, :].rearrange("p (h q) -> p h q", h=16),
                    in0=psb(ptx)[:, :].rearrange("p (h q) -> p h q", h=16),
                    in1=dtt[:, :].unsqueeze(2).to_broadcast([128, 16, 64]), op=ALU.mult),
                    reads=[PS[ptx], dtt], writes=[xdt])
                fw.op("pool", lambda e: e.tensor_tensor(
                    out=xdd[:, :].rearrange("p (h q) -> p h q", h=16),
                    in0=xdt[:, :].rearrange("p (h q) -> p h q", h=16),
                    in1=dstate[:, :].unsqueeze(2).to_broadcast([128, 16, 64]), op=ALU.mult),
                    reads=[xdt, dstate], writes=[xdd])
                ptb = next_pt()
                for j in range(2):
                    fw.op("pe", lambda e, j=j: e.transpose(out=psb(ptb)[:, j * 128:(j + 1) * 128], in_=x_c[:, 8 + j, :],
                                                           identity=identb[:, :]),
                          reads=[x_c, identb], writes=[PS[ptb]], signal=(j == 1))
                fw.op("act", lambda e: e.activation(out=B_tm[:, :], in_=psb(ptb)[:, 0:256], func=AF.Copy),
                      reads=[PS[ptb]], writes=[B_tm])

            def A_s3(c, i):
                main = c >= NCH
                m = c - NCH
                x_c = x_cs[i % 2]
                adt = adts[i % 2]; nacs = nacss[i % 2]; cdt = cdts[i % 2]; expacs = expacss[i % 2]
                xdt = xdts[i % 2]; xdd = xdds[i % 2]; B_tm = B_tms[i % 2]
                gz = gzs[i % 2]
                MT = MT_l[i % 2]; yt = yt_l[i % 2]; yn = yn_l[i % 2]; ss = ss_l[i % 2]; ve2 = ve2_l[i % 2]
                ti2 = ti2_l[i % 2]; y2 = y2_l[i % 2]; tb2 = tb2_l[i % 2]
                if main:
                    pcb = 5
                    for g in range(2):
                        fw.op("pe", lambda e, g=g: e.matmul(PS[pcb][:, g * 128:(g + 1) * 128], lhsT=x_c[:, 8 + g, :],
                                                            rhs=x_c[:, 10 + g, :], start=True, stop=True),
                              reads=[x_c], writes=[PS[pcb]], signal=(g == 1))
                    fw.op("act", lambda e: e.activation(out=cbs_t[:, 0:256], in_=PS[pcb][:, 0:256], func=AF.Copy),
                          reads=[PS[pcb]], writes=[cbs_t])
                    for q in range(4):
                        bq = 4 + (q % 2)
                        g = q // 2
                        fw.op("pe", lambda e, bq=bq: e.matmul(
                            PS[bq][:, :], lhsT=identb[:, :], rhs=NEGM[:, :, :].rearrange("p a b -> p (a b)"),
                            start=True, stop=False), reads=[identb, NEGM], writes=[PS[bq]], signal=False)
                        if BC_HILO:
                            adt_hi = adt_his[i % 2]; adt_lo = adt_los[i % 2]
                            for j in range(4):
                                h = 4 * q + j
                                fw.op("pe", lambda e, j=j, h=h, bq=bq: e.matmul(
                                    PS[bq][:, j * 128:(j + 1) * 128], lhsT=adt_hi[:, h:h + 1].to_broadcast([128, 128]),
                                    rhs=Ub[:, :], start=False, stop=False), reads=[adt_hi, Ub], writes=[PS[bq]], signal=False)
                                fw.op("pe", lambda e, j=j, h=h, bq=bq: e.matmul(
                                    PS[bq][:, j * 128:(j + 1) * 128], lhsT=adt_lo[:, h:h + 1].to_broadcast([128, 128]),
                                    rhs=Ub[:, :], start=False, stop=(j == 3)), reads=[adt_lo, Ub], writes=[PS[bq]],
                                    signal=(j == 3))
                        else:
                            for j in range(4):
                                h = 4 * q + j
                                fw.op("pe", lambda e, j=j, h=h, bq=bq: e.matmul(
                                    PS[bq][:, j * 128:(j + 1) * 128], lhsT=adt[:, h:h + 1].to_broadcast([128, 128]),
                                    rhs=U[:, :], start=False, stop=(j == 3)), reads=[adt, U], writes=[PS[bq]],
                                    signal=(j == 3))
                        lqt = lq[q % 2]
                        for j in range(4):
                            h = 4 * q + j
                            fw.op("act", lambda e, j=j, h=h, bq=bq, lqt=lqt: e.activation(
                                out=lqt[:, j * 128:(j + 1) * 128], in_=PS[bq][:, j * 128:(j + 1) * 128], func=AF.Exp,
                                bias=nacs[:, h:h + 1]), reads=[PS[bq], nacs], writes=[lqt])
                        fw.op(MT_ENG, lambda e, q=q, g=g, lqt=lqt: e.tensor_tensor(
                            out=MT[:, 4 * q:4 * q + 4, :], in0=lqt[:, :].rearrange("p (a b) -> p a b", a=4),
                            in1=cbs_t[:, g * 128:(g + 1) * 128].unsqueeze(1).to_broadcast([128, 4, 128]), op=ALU.mult),
                            reads=[lqt, cbs_t], writes=[MT])
                    for h in range(16):
                        g = h // 8
                        yb = 6 + g
                        o = (h % 8) * 64
                        ct = h // 2
                        co = (h % 2) * 64
                        fw.op("pe", lambda e, yb=yb, o=o, ct=ct, co=co: e.matmul(
                            PS[yb][:, o:o + 64], lhsT=x_c[:, ct, :], rhs=dDhi[:, ct, co:co + 64], start=True, stop=False),
                            reads=[x_c, dDhi], writes=[PS[yb]], signal=False)
                        fw.op("pe", lambda e, yb=yb, o=o, ct=ct, co=co: e.matmul(
                            PS[yb][:, o:o + 64], lhsT=x_c[:, ct, :], rhs=dDlo[:, ct, co:co + 64], start=False, stop=False),
                            reads=[x_c, dDlo], writes=[PS[yb]], signal=False)
                        fw.op("pe", lambda e, yb=yb, o=o, h=h: e.matmul(
                            PS[yb][:, o:o + 64], lhsT=MT[:, h, :], rhs=xdt[:, h * 64:(h + 1) * 64], start=False, stop=True),
                            reads=[MT, xdt], writes=[PS[yb]], signal=(h % 8 == 7))
                    for g in range(2):
                        fw.op("pe", lambda e, g=g: e.matmul(PS[g][:, :], lhsT=x_c[:, 10 + g, :],
                                                            rhs=Sbf[:, g * 512:(g + 1) * 512], start=True, stop=True),
                              reads=[x_c, Sbf], writes=[PS[g]])
                        fw.op("dve", lambda e, g=g: e.tensor_tensor(
                            out=yt[:, g * 512:(g + 1) * 512].rearrange("p (h q) -> p h q", h=8),
                            in0=PS[g][:, :].rearrange("p (h q) -> p h q", h=8),
                            in1=expacs[:, g * 8:(g + 1) * 8].unsqueeze(2).to_broadcast([128, 8, 64]), op=ALU.mult),
                            reads=[PS[g], expacs], writes=[yt])
                        fw.op("dve", lambda e, g=g: e.tensor_tensor(
                            out=yt[:, g * 512:(g + 1) * 512], in0=yt[:, g * 512:(g + 1) * 512], in1=PS[6 + g][:, :],
                            op=ALU.add), reads=[yt, PS[6 + g]], writes=[yt])
                    fw.op("pool", lambda e: e.tensor_tensor(out=yt[:, :], in0=yt[:, :], in1=gz[:, :], op=ALU.mult),
                          reads=[yt, gz], writes=[yt])
                    fw.op("pool", lambda e: e.memset(ss[:, :], 0.0), writes=[ss])
                    for g in range(2):
                        junk = lq[g]
                        fw.op("act", lambda e, g=g, junk=junk: e.activation(out=junk[:, :], in_=yt[:, g * 512:(g + 1) * 512],
                                                                            func=AF.Square, accum_out=ss[:, g:g + 1]),
                              reads=[yt, ss], writes=[junk, ss])
                    fw.op("dve", lambda e: e.tensor_scalar(out=ve2[:, :], in0=ss[:, :], scalar1=1.0 / 512.0,
                                                           scalar2=4.0 * RMS_EPS, op0=ALU.mult, op1=ALU.add),
                          reads=[ss], writes=[ve2])
                    r2 = rsqrt_chain((ti2, y2, tb2), ve2, 2)
                    for g in range(2):
                        fw.op("act", lambda e, g=g: e.activation(out=yn[:, g * 512:(g + 1) * 512],
                                                                 in_=yt[:, g * 512:(g + 1) * 512], func=AF.Identity,
                                                                 scale=r2[:, g:g + 1]), reads=[yt, r2], writes=[yn])
                    pty = next_pt()
                    for k in range(8):
                        fw.op("pe", lambda e, k=k: e.transpose(out=psb(pty)[:, k * 128:(k + 1) * 128],
                                                               in_=yn[:, k * 128:(k + 1) * 128], identity=identb[:, :]),
                              reads=[yn, identb], writes=[PS[pty]], signal=(k == 7))
                    fw.op("act", lambda e, m=m: e.activation(
                        out=YS[:, :, m * 128:(m + 1) * 128],
                        in_=psb(pty)[:, :].rearrange("p (a b) -> p a b", a=8), func=AF.Copy),
                        reads=[PS[pty]], writes=[YS])
                if c < 2 * NCH - 1:
                    fw.op("pool", lambda e: e.tensor_tensor(
                        out=S[:, :].rearrange("p (h q) -> p h q", h=16),
                        in0=S[:, :].rearrange("p (h q) -> p h q", h=16),
                        in1=cdt[:, :].unsqueeze(2).to_broadcast([128, 16, 64]), op=ALU.mult),
                        reads=[S, cdt], writes=[S])
                    for g in range(2):
                        bs = 2 + g
                        fw.op("pe", lambda e, g=g, bs=bs: e.matmul(PS[bs][:, :], lhsT=B_tm[:, g * 128:(g + 1) * 128],
                                                                   rhs=xdd[:, g * 512:(g + 1) * 512], start=True, stop=True),
                              reads=[B_tm, xdd], writes=[PS[bs]])
                        fw.op("dve", lambda e, g=g, bs=bs: e.tensor_tensor(
                            out=S[:, g * 512:(g + 1) * 512], in0=S[:, g * 512:(g + 1) * 512], in1=PS[bs][:, :], op=ALU.add),
                            reads=[S, PS[bs]], writes=[S])
                    if c == NCH - 1:
                        fw.op("pool", lambda e: e.tensor_scalar_mul(out=S[:, :], in0=S[:, :], scalar1=flag[:, 0:1]),
                              reads=[S, flag], writes=[S])
                    fw.op("act", lambda e: e.activation(out=Sbf[:, :], in_=S[:, :], func=AF.Copy), reads=[S], writes=[Sbf])

            evac_eng[0] = "dve"
            NEWTON_CUR[0] = NEWTON_A
            pipeline([A_s0, A_s1, A_s2, A_s3], list(range(2 * NCH)))
            if PASS_BARRIER:
                fw.barrier()

        if debug:
            with ExitStack() as sd:
                dtile = fw.sb(sd, [128, 8 * NM], F32, "dtile")
                fw.op("dve", lambda e: e.tensor_copy(out=dtile[:, :], in_=YS[:, :, :].rearrange("p a b -> p (a b)")),
                      reads=[YS], writes=[dtile])
                fw.store("sp", dtile, dbg_ys[:, :], dtile[:, :])
                dtile2 = fw.sb(sd, [128, 8 * NM], F32, "dtile2")
                fw.op("dve", lambda e: e.tensor_copy(out=dtile2[:, :], in_=YC[:, :, :].rearrange("p a b -> p (a b)")),
                      reads=[YC], writes=[dtile2])
                fw.store("sp", dtile2, dbg_yc[:, :], dtile2[:, :])
                fw.barrier()

        with ExitStack() as sc:
            ngfm = fw.sb(sc, [128, 8], F32, "ngfm")
            g1fm = fw.sb(sc, [128, 8], F32, "g1fm")
            b1fm = fw.sb(sc, [128, 8], F32, "b1fm")
            G0 = fw.sb(sc, [128, 1024], F32, "G0")
            G1 = fw.sb(sc, [128, 1024], F32, "G1")
            B1 = fw.sb(sc, [128, 1024], F32, "B1")
            G2 = fw.sb(sc, [128, 1024], F32, "G2")
            B2 = fw.sb(sc, [128, 1024], F32, "B2")
            B0rows = fw.sb(sc, [33, 1024], BF16, "B0rows")
            fw.load("sp", ngfm, ngfm[:, :], ngfm_d[:, :])
            fw.load("sp", g1fm, g1fm[:, :], g1fm_d[:, :])
            fw.load("sp", b1fm, b1fm[:, :], b1fm_d[:, :])
            fw.op("pool", lambda e: e.memset(B0rows[:, :], 0.0), writes=[B0rows])
            Wo = WBlocks(sc, "Wo", w_out, 16, [(0, 512), (512, 512)])
            Wg = WBlocks(sc, "Wg", w_gate, 8, [(0, 512), (512, 512)])
            Wp = WBlocks(sc, "Wp", w_ple, 2, [(0, 1024)])
            for nb in range(2):
                wt_ = Wo.tiles[nb]
                for k in range(8):
                    if (k + nb) % 2 == 0:
                        fw.op("act", lambda e, k=k, wt_=wt_: e.activation(out=wt_[:, k, :], in_=wt_[:, k, :], func=AF.Copy,
                                                                          scale=ngfm[:, k:k + 1]), reads=[wt_, ngfm], writes=[wt_])
                    else:
                        fw.op("dve", lambda e, k=k, wt_=wt_: e.tensor_scalar_mul(out=wt_[:, k, :], in0=wt_[:, k, :],
                                                                                 scalar1=ngfm[:, k:k + 1]),
                              reads=[wt_, ngfm], writes=[wt_])
            hilo_rows(sc, B0rows, [(ALPHA, b0row_d), (1.0, boutrow_d)], 1024, CH=512)
            fw.load("sp", G0, G0[:, :], g0bc_d[:, :])
            fw.load("sp", G1, G1[:, :], g1bc_d[:, :])
            fw.load("sp", B1, B1[:, :], b1bc_d[:, :])
            fw.load("sp", G2, G2[:, :], g2bc_d[:, :])
            fw.load("sp", B2, B2[:, :], b2bc_d[:, :])
            fw.op("pool", lambda e: e.tensor_scalar_mul(out=G0[:, :], in0=G0[:, :], scalar1=ALPHA), reads=[G0], writes=[G0])
            fw.op("pool", lambda e: e.tensor_scalar_mul(out=G1[:, :], in0=G1[:, :], scalar1=ALPHA), reads=[G1], writes=[G1])
            fw.op("pool", lambda e: e.tensor_scalar_mul(out=B1[:, :], in0=B1[:, :], scalar1=ALPHA), reads=[B1], writes=[B1])

            lnsC = LNS(sc, "lnC", nsets=6)
            lnsC.engs = LNC_ENGS
            xins = [fw.sb(sc, [128, 1024], F32, "xinC") for _ in range(2)]
            pins = [fw.sb(sc, [128, 256], F32, "pin") for _ in range(2)]
            r0s = [fw.sb(sc, [128, 1024], F32, "r0") for _ in range(2)]
            n1s = [fw.sb(sc, [128, 1024], F32, "n1") for _ in range(2)]
            n1b = fw.sb(sc, [128, 1024], BF16, "n1b")
            h1Ts = [mk_hT(sc, "h1T") for _ in range(2)]
            pb = fw.sb(sc, [128, 256], BF16, "pb")
            pTs = [fw.sb(sc, [128, 2, 128], BF16, "pT") for _ in range(2)]
            tgate = [fw.sb(sc, [128, 512], F32, "tgate") for _ in range(2)]
            r2t = fw.sb(sc, [128, 1024], F32, "r2t")
            outs = [fw.sb(sc, [128, 1024], F32, "outt") for _ in range(2)]
            out_toks = []

            def C_s0(m, i):
                xin = xins[i % 2]
                pin = pins[i % 2]
                r0 = r0s[i % 2]
                pT = pTs[i % 2]
                fw.load("sp", xin, xin[:, :], xa[(NCH + m) * 128:(NCH + m + 1) * 128, :])
                fw.load("sp", pin, pin[:, :], pa[m * 128:(m + 1) * 128, :])
                rstd, nmr = lnsC.run(xin)
                fw.op("act", lambda e: e.activation(out=r0[:, :], in_=xin[:, :], func=AF.Identity, bias=nmr[:, :],
                                                    scale=rstd[:, :]), reads=[xin, nmr, rstd], writes=[r0])
                fw.op(C_TT, lambda e: e.tensor_tensor(out=r0[:, :], in0=r0[:, :], in1=G0[:, :], op=ALU.mult),
                      reads=[r0, G0], writes=[r0])
                if C_PB == "act":
                    fw.op("act", lambda e: e.activation(out=pb[:, :], in_=pin[:, :], func=AF.Copy), reads=[pin], writes=[pb])
                else:
                    fw.op("pool", lambda e: e.tensor_copy(out=pb[:, :], in_=pin[:, :]), reads=[pin], writes=[pb])
                pt2 = next_pt()
                for k in range(2):
                    fw.op("pe", lambda e, k=k: e.transpose(out=psb(pt2)[:, k * 128:(k + 1) * 128],
                                                           in_=pb[:, k * 128:(k + 1) * 128], identity=identb[:, :]),
                          reads=[pb, identb], writes=[PS[pt2]], signal=(k == 1))
                fw.op("dve", lambda e: e.tensor_copy(out=pT[:, :, :],
                                                     in_=psb(pt2)[:, 0:256].rearrange("p (a b) -> p a b", a=2)),
                      reads=[PS[pt2]], writes=[pT])

            def C_s1(m, i):
                r0 = r0s[i % 2]
                n1 = n1s[i % 2]
                h1T = h1Ts[i % 2]
                for nb in range(2):
                    fw.op("pe", lambda e, nb=nb: e.matmul(PS[nb][:, :], lhsT=ones33[:, :],
                                                          rhs=B0rows[:, nb * 512:(nb + 1) * 512], start=True, stop=False),
                          reads=[ones33, B0rows], writes=[PS[nb]], signal=False)
                    for kt in range(16):
                        src = YS if kt < 8 else YC
                        fw.op("pe", lambda e, nb=nb, kt=kt, src=src: e.matmul(
                            PS[nb][:, :], lhsT=src[:, kt % 8, m * 128:(m + 1) * 128],
                            rhs=Wo.get(kt, nb * 512, 512)[1],
                            start=False, stop=(kt == 15)), reads=[src, Wo.get(kt, nb * 512, 512)[0]], writes=[PS[nb]],
                            signal=(kt == 15))
                    fw.op("dve", lambda e, nb=nb: e.tensor_tensor(out=r0[:, nb * 512:(nb + 1) * 512],
                                                                  in0=r0[:, nb * 512:(nb + 1) * 512], in1=PS[nb][:, :],
                                                                  op=ALU.add), reads=[r0, PS[nb]], writes=[r0])
                rstd1, nmr1 = lnsC.run(r0)
                fw.op("act", lambda e: e.activation(out=n1b[:, :], in_=r0[:, :], func=AF.Identity, bias=nmr1[:, :],
                                                    scale=rstd1[:, :]), reads=[r0, nmr1, rstd1], writes=[n1b])
                fw.op("act", lambda e: e.activation(out=n1[:, :], in_=r0[:, :], func=AF.Identity, bias=nmr1[:, :],
                                                    scale=rstd1[:, :]), reads=[r0, nmr1, rstd1], writes=[n1])
                pt = next_pt()
                for k in range(8):
                    fw.op("pe", lambda e, k=k: e.transpose(out=psb(pt)[:, k * 128:(k + 1) * 128],
                                                           in_=n1b[:, k * 128:(k + 1) * 128], identity=identb[:, :]),
                          reads=[n1b, identb], writes=[PS[pt]], signal=(k == 7))
                evac_T(pt, h1T, g1fm, b1fm)
                fw.op(C_TT, lambda e: e.tensor_tensor(out=n1[:, :], in0=n1[:, :], in1=G1[:, :], op=ALU.mult),
                      reads=[n1, G1], writes=[n1])
                fw.op("pool", lambda e: e.tensor_tensor(out=n1[:, :], in0=n1[:, :], in1=B1[:, :], op=ALU.add),
                      reads=[n1, B1], writes=[n1])

            def C_s2(m, i):
                n1 = n1s[i % 2]
                h1T = h1Ts[i % 2]
                pT = pTs[i % 2]
                for nb in range(2):
                    bgk = 4 + nb
                    for kt in range(8):
                        fw.op("pe", lambda e, nb=nb, kt=kt, bgk=bgk: e.matmul(
                            PS[bgk][:, :], lhsT=hsl(h1T, kt), rhs=Wg.get(kt, nb * 512, 512)[1],
                            start=(kt == 0), stop=(kt == 7)), reads=[h1T, Wg.get(kt, nb * 512, 512)[0]], writes=[PS[bgk]],
                            signal=(kt == 7))
                    tgt = tgate[nb]
                    fw.op("act", lambda e, bgk=bgk, tgt=tgt: e.activation(out=tgt[:, :], in_=PS[bgk][:, :], func=AF.Tanh,
                                                                          scale=0.5), reads=[PS[bgk]], writes=[tgt])
                    bpk = 6 + nb
                    for kt in range(2):
                        fw.op("pe", lambda e, nb=nb, kt=kt, bpk=bpk: e.matmul(
                            PS[bpk][:, :], lhsT=pT[:, kt, :], rhs=Wp.get(kt, nb * 512, 512)[1],
                            start=(kt == 0), stop=(kt == 1)), reads=[pT, Wp.tiles[0]], writes=[PS[bpk]], signal=(kt == 1))
                    fw.op("dve", lambda e, nb=nb, bpk=bpk, tgt=tgt: e.scalar_tensor_tensor(
                        out=r2t[:, nb * 512:(nb + 1) * 512], in0=tgt[:, :], scalar=1.0, in1=PS[bpk][:, :],
                        op0=ALU.add, op1=ALU.mult), reads=[tgt, PS[bpk]], writes=[r2t])
                    fw.op("dve", lambda e, nb=nb: e.scalar_tensor_tensor(
                        out=r2t[:, nb * 512:(nb + 1) * 512], in0=r2t[:, nb * 512:(nb + 1) * 512], scalar=0.5,
                        in1=n1[:, nb * 512:(nb + 1) * 512], op0=ALU.mult, op1=ALU.add), reads=[r2t, n1], writes=[r2t])
                rstd2, nmr2 = lnsC.run(r2t)
                ot = outs[i % 2]
                fw.op("act", lambda e, ot=ot: e.activation(out=ot[:, :], in_=r2t[:, :], func=AF.Identity, bias=nmr2[:, :],
                                                           scale=rstd2[:, :]), reads=[r2t, nmr2, rstd2], writes=[ot])
                fw.op("pool", lambda e, ot=ot: e.tensor_tensor(out=ot[:, :], in0=ot[:, :], in1=G2[:, :], op=ALU.mult),
                      reads=[ot, G2], writes=[ot])
                fw.op(C_B2, lambda e, ot=ot: e.tensor_tensor(out=ot[:, :], in0=ot[:, :], in1=B2[:, :], op=ALU.add),
                      reads=[ot, B2], writes=[ot])
                out_toks.append(fw.store("sp", ot, out_d[m * 128:(m + 1) * 128, :], ot[:, :]))

            evac_eng[0] = EVAC_BC
            pipeline([C_s0, C_s1, C_s2], list(range(NCH)))
            fw.wait_all("sp", out_toks)
            fw.barrier()
    return nc


def _fm(v, nt):
    return np.ascontiguousarray(np.asarray(v, dtype=np.float32).reshape(nt, 128).T)


def _bc(v):
    v = np.asarray(v, dtype=np.float32).reshape(1, -1)
    return np.ascontiguousarray(np.broadcast_to(v, (128, v.shape[1])))


def make_core_inputs(inputs, b, hf, NCH=NCH_FULL):
    half = NCH * 128
    x = np.asarray(inputs["x"], dtype=np.float32)
    p = np.asarray(inputs["p"], dtype=np.float32)
    if hf == 0:
        xa = np.concatenate([np.zeros((half, D_MODEL), np.float32), x[b, 0:half]], axis=0)
        pa = p[0, b, 0:half]
        fl = 0.0
    else:
        xa = x[b, 0:2 * half]
        pa = p[0, b, half:2 * half]
        fl = 1.0
    c4w = np.asarray(inputs["ssm_conv_w"], np.float32)[0]
    c31w = np.asarray(inputs["conf_conv_w"], np.float32)[0]
    d = {
        "xa": np.ascontiguousarray(xa), "pa": np.ascontiguousarray(pa),
        "flag": np.full((128, 1), fl, np.float32),
        "w_in": np.ascontiguousarray(np.asarray(inputs["w_in"], np.float32)[0]),
        "w_out": np.ascontiguousarray(np.asarray(inputs["w_out"], np.float32)[0]),
        "w_gate": np.ascontiguousarray(np.asarray(inputs["w_ple_gate"], np.float32)[0]),
        "w_ple": np.ascontiguousarray(np.asarray(inputs["w_ple_proj"], np.float32)[0]),
        "g0fm": _fm(inputs["ln_emb_g"], 8), "b0fm": _fm(inputs["ln_emb_b"], 8),
        "g0bc": _bc(inputs["ln_emb_g"]),
        "b0row": np.ascontiguousarray(np.asarray(inputs["ln_emb_b"], np.float32).reshape(1, 1024)),
        "boutrow": np.ascontiguousarray(np.asarray(inputs["b_out"], np.float32).reshape(1, 1024)),
        "c4w": np.ascontiguousarray(c4w.T.reshape(12, 128, 4).transpose(1, 0, 2)),
        "c4b": _fm(np.asarray(inputs["ssm_conv_b"])[0], 12),
        "dtb": _bc(inputs["dt_bias"]), "alog": _bc(inputs["a_log"]),
        "dfm": _fm(np.repeat(np.asarray(inputs["d_skip"], np.float32).reshape(16), 64), 8),
        "ngfm": _fm(np.asarray(inputs["ssm_norm_g"])[0], 8),
        "bglu": np.ascontiguousarray(np.asarray(inputs["b_glu"], np.float32).reshape(1, 2048)),
        "c31w": np.ascontiguousarray(c31w.T.reshape(8, 128, 31).transpose(1, 0, 2)),
        "c31b": _fm(np.asarray(inputs["conf_conv_b"])[0], 8),
        "clg": _fm(np.asarray(inputs["conf_ln_g"])[0], 8), "clb": _fm(np.asarray(inputs["conf_ln_b"])[0], 8),
        "g1fm": _fm(np.asarray(inputs["ln1_g"])[0], 8), "b1fm": _fm(np.asarray(inputs["ln1_b"])[0], 8),
        "g1bc": _bc(inputs["ln1_g"]), "b1bc": _bc(inputs["ln1_b"]),
        "g2bc": _bc(inputs["ln2_g"]), "b2bc": _bc(inputs["ln2_b"]),
    }
    return d


_NC_CACHE = {}


def kernel(**inputs):
    if "full" not in _NC_CACHE:
        _NC_CACHE["full"] = build(NCH_FULL)
    nc = _NC_CACHE["full"]
    in_maps = []
    for core in range(8):
        b, hf = core // 2, core % 2
        in_maps.append(make_core_inputs(inputs, b, hf))
    res = run_bass_kernel_spmd(nc, in_maps, core_ids=list(range(8)))
    out = np.zeros((BATCH, SEQ, D_MODEL), np.float32)
    for core in range(8):
        b, hf = core // 2, core % 2
        out[b, hf * 2048:(hf + 1) * 2048] = res.results[core]["out"]
    return out
```
